# Optimizing a Trainium2 kernel written in Bass

```python
import jax, jax.numpy as jnp
from jax import lax
import numpy as np

D_MODEL = 1024
BATCH = 8
SEQ = 2048
DEPTH = 2

HEAD_DIM = 64
FOX_HEADS = 4
RWKV_HEADS = 8
MOBA_HEADS = 4
FOX_W = FOX_HEADS * HEAD_DIM
RWKV_W = RWKV_HEADS * HEAD_DIM
MOBA_W = MOBA_HEADS * HEAD_DIM
D_MIX = FOX_W + RWKV_W + MOBA_W
DECAY_LORA = 64
AAA_LORA = 64
GATE_LORA = 128
MOBA_BLOCK = 256
MOBA_TOPK = 3
FOX_Q_BLOCK = 128
MOBA_Q_BLOCK = 64
D_FF = 2816
CONV_W = 3
NORM_EPS = 1e-6
GN_EPS = 64e-5

FOX_COLS = 3 * FOX_W + FOX_HEADS
RWKV_COLS = 3 * RWKV_W + DECAY_LORA + AAA_LORA + GATE_LORA
MOBA_COLS = 3 * MOBA_W
D_IN = FOX_COLS + RWKV_COLS + MOBA_COLS
RWKV_SPLITS = [RWKV_W, 2 * RWKV_W, 3 * RWKV_W, 3 * RWKV_W + DECAY_LORA, 3 * RWKV_W + DECAY_LORA + AAA_LORA]

kernel_name = "hymba_style_fox_rwkv7_moba_hybrid"


def _rmsnorm(x, g):
    xf = x.astype(jnp.float32)
    y = xf * lax.rsqrt(jnp.mean(xf * xf, axis=-1, keepdims=True) + NORM_EPS)
    return (y * g).astype(x.dtype)


def _split_heads(t, n_heads):
    b, s, _ = t.shape
    return t.reshape(b, s, n_heads, HEAD_DIM).transpose(0, 2, 1, 3)


def _merge_heads(t):
    b, h, s, d = t.shape
    return t.transpose(0, 2, 1, 3).reshape(b, s, h * d)


def _shift_right(t, n):
    return jnp.pad(t, ((0, 0), (n, 0), (0, 0)))[:, : t.shape[1]]


def fox_attention(q, k, v, f_logit, f_bias):
    b, s, _ = q.shape
    q, k, v = (_split_heads(t, FOX_HEADS) for t in (q, k, v))
    log_f = jax.nn.log_sigmoid(f_logit.astype(jnp.float32) + f_bias.astype(jnp.float32))
    cum = jnp.cumsum(log_f, axis=1).transpose(0, 2, 1)
    n_blk = s // FOX_Q_BLOCK
    q_blocks = q.reshape(b, FOX_HEADS, n_blk, FOX_Q_BLOCK, HEAD_DIM).transpose(2, 0, 1, 3, 4)
    cum_blocks = cum.reshape(b, FOX_HEADS, n_blk, FOX_Q_BLOCK).transpose(2, 0, 1, 3)
    k_pos = jnp.arange(s)
    scale = HEAD_DIM ** -0.5

    def one_block(args):
        q_i, cum_i, i = args
        q_pos = i * FOX_Q_BLOCK + jnp.arange(FOX_Q_BLOCK)
        logits = jnp.einsum('bhqd,bhkd->bhqk', q_i, k).astype(jnp.float32) * scale
        logits = logits + cum_i[..., :, None] - cum[..., None, :]
        logits = jnp.where(k_pos[None, :] <= q_pos[:, None], logits, -jnp.inf)
        probs = jax.nn.softmax(logits, axis=-1).astype(v.dtype)
        return jnp.einsum('bhqk,bhkd->bhqd', probs, v)

    out = lax.map(one_block, (q_blocks, cum_blocks, jnp.arange(n_blk)))
    return _merge_heads(out.transpose(1, 2, 0, 3, 4).reshape(b, FOX_HEADS, s, HEAD_DIM))


def rwkv7_time_mix(feat, mu, w0, w2, a0, a2, g2, k_k, k_a, r_k, ln_w, ln_b):
    b, s, _ = feat.shape
    feat = feat + (_shift_right(feat, 1) - feat) * mu
    r, k, v, w_lo, a_lo, g_lo = jnp.split(feat, RWKV_SPLITS, axis=-1)
    log_w = -jax.nn.softplus(-(w0 + jnp.tanh(w_lo) @ w2)) - 0.5
    decay = jnp.exp(-jnp.exp(log_w.astype(jnp.float32)))
    a = jax.nn.sigmoid(a0 + a_lo @ a2)
    g = jax.nn.sigmoid(g_lo) @ g2
    kk = (k * k_k).astype(jnp.float32).reshape(b, s, RWKV_HEADS, HEAD_DIM)
    kk = kk / jnp.maximum(jnp.sqrt(jnp.sum(kk * kk, axis=-1, keepdims=True)), 1e-12)
    k = k * (1.0 + (a - 1.0) * k_a)

    def heads(t):
        return t.astype(jnp.float32).reshape(b, s, RWKV_HEADS, HEAD_DIM)

    rh, kh, vh, ah, wh = heads(r), heads(k), heads(v), heads(a), heads(decay)
    xs = tuple(t.transpose(1, 0, 2, 3) for t in (rh, wh, kh, vh, kk, ah))

    def step(state, inp):
        r_t, w_t, k_t, v_t, kk_t, a_t = inp
        removed = jnp.einsum('bhvk,bhk->bhv', state, kk_t)
        state = (state * w_t[:, :, None, :]
                 - removed[..., None] * (kk_t * a_t)[:, :, None, :]
                 + v_t[..., None] * k_t[:, :, None, :])
        return state, jnp.einsum('bhvk,bhk->bhv', state, r_t)

    state0 = jnp.zeros((b, RWKV_HEADS, HEAD_DIM, HEAD_DIM), jnp.float32)
    _, y = lax.scan(step, state0, xs)
    y = y.transpose(1, 0, 2, 3)
    mean = jnp.mean(y, axis=-1, keepdims=True)
    var = jnp.mean(jnp.square(y - mean), axis=-1, keepdims=True)
    y = ((y - mean) * lax.rsqrt(var + GN_EPS)).reshape(b, s, RWKV_W) * ln_w + ln_b
    bonus = jnp.sum(rh * kh * r_k, axis=-1, keepdims=True) * vh
    y = (y + bonus.reshape(b, s, RWKV_W)) * g
    return y.astype(feat.dtype)


def moba_attention(q, k, v):
    b, s, _ = q.shape
    q, k, v = (_split_heads(t, MOBA_HEADS) for t in (q, k, v))
    n_kb = -(-s // MOBA_BLOCK)
    s_pad = n_kb * MOBA_BLOCK
    pad = ((0, 0), (0, 0), (0, s_pad - s), (0, 0))
    k_p, v_p = jnp.pad(k, pad), jnp.pad(v, pad)
    k_blocks = k_p.reshape(b, MOBA_HEADS, n_kb, MOBA_BLOCK, HEAD_DIM)
    v_blocks = v_p.reshape(b, MOBA_HEADS, n_kb, MOBA_BLOCK, HEAD_DIM)
    k_mean = jnp.mean(k_blocks.astype(jnp.float32), axis=3)
    top_k = min(MOBA_TOPK, n_kb)
    n_qc = s // MOBA_Q_BLOCK
    q_chunks = q.reshape(b, MOBA_HEADS, n_qc, MOBA_Q_BLOCK, HEAD_DIM).transpose(2, 0, 1, 3, 4)
    gather_blocks = jax.vmap(jax.vmap(lambda blocks, idx: blocks[idx]))
    blk_ids = jnp.arange(n_kb)
    own_offsets = jnp.arange(MOBA_BLOCK)
    scale = HEAD_DIM ** -0.5

    def one_chunk(args):
        q_i, ci = args
        q_pos = ci * MOBA_Q_BLOCK + jnp.arange(MOBA_Q_BLOCK)
        own = (ci * MOBA_Q_BLOCK) // MOBA_BLOCK
        gate = jnp.einsum('bhqd,bhnd->bhqn', q_i.astype(jnp.float32), k_mean)
        gate = jnp.where(blk_ids < own, gate, -jnp.inf)
        gate_val, sel = lax.top_k(gate, top_k)
        valid = jnp.isfinite(gate_val)
        k_sel = gather_blocks(k_blocks, sel)
        v_sel = gather_blocks(v_blocks, sel)
        s_sel = jnp.einsum('bhqd,bhqnkd->bhqnk', q_i, k_sel).astype(jnp.float32) * scale
        s_sel = jnp.where(valid[..., None], s_sel, -jnp.inf)
        k_own = lax.dynamic_slice_in_dim(k_p, own * MOBA_BLOCK, MOBA_BLOCK, axis=2)
        v_own = lax.dynamic_slice_in_dim(v_p, own * MOBA_BLOCK, MOBA_BLOCK, axis=2)
        s_own = jnp.einsum('bhqd,bhkd->bhqk', q_i, k_own).astype(jnp.float32) * scale
        s_own = jnp.where(own * MOBA_BLOCK + own_offsets[None, :] <= q_pos[:, None], s_own, -jnp.inf)
        logits = jnp.concatenate([s_sel.reshape(b, MOBA_HEADS, MOBA_Q_BLOCK, top_k * MOBA_BLOCK), s_own], axis=-1)
        probs = jax.nn.softmax(logits, axis=-1).astype(v.dtype)
        p_sel = probs[..., : top_k * MOBA_BLOCK].reshape(b, MOBA_HEADS, MOBA_Q_BLOCK, top_k, MOBA_BLOCK)
        p_own = probs[..., top_k * MOBA_BLOCK:]
        return (jnp.einsum('bhqnk,bhqnkd->bhqd', p_sel, v_sel)
                + jnp.einsum('bhqk,bhkd->bhqd', p_own, v_own))

    out = lax.map(one_chunk, (q_chunks, jnp.arange(n_qc)))
    return _merge_heads(out.transpose(1, 2, 0, 3, 4).reshape(b, MOBA_HEADS, s, HEAD_DIM))


def _causal_dwconv(u, w, bias):
    out = bias
    for j in range(CONV_W):
        out = out + _shift_right(u, CONV_W - 1 - j) * w[j]
    return out


def setup_inputs(seed: int = 0) -> dict:
    key = jax.random.key(seed)
    ks = iter(jax.random.split(key, 32))

    def nrm(shape, scale):
        return scale * jax.random.normal(next(ks), shape, jnp.float32)

    L, D = DEPTH, D_MODEL
    return {
        "x": nrm((BATCH, SEQ, D), 1.0),
        "c": nrm((BATCH, D), 1.0),
        "w_mod": nrm((L, D, 6 * D), 0.5 * D ** -0.5),
        "b_mod": nrm((L, 6 * D), 0.02),
        "norm_mix": 1.0 + nrm((L, D), 0.02),
        "w_in": nrm((L, D, D_IN), D ** -0.5),
        "fox_f_bias": 2.0 + nrm((L, FOX_HEADS), 0.1),
        "rwkv_mu": jax.random.uniform(next(ks), (L, RWKV_COLS), jnp.float32),
        "rwkv_w0": -1.0 + nrm((L, RWKV_W), 0.5),
        "rwkv_w2": nrm((L, DECAY_LORA, RWKV_W), 0.5 * DECAY_LORA ** -0.5),
        "rwkv_a0": nrm((L, RWKV_W), 0.1),
        "rwkv_a2": nrm((L, AAA_LORA, RWKV_W), AAA_LORA ** -0.5),
        "rwkv_g2": nrm((L, GATE_LORA, RWKV_W), GATE_LORA ** -0.5),
        "rwkv_k_k": 0.85 + nrm((L, RWKV_W), 0.05),
        "rwkv_k_a": 1.0 + nrm((L, RWKV_W), 0.05),
        "rwkv_r_k": nrm((L, RWKV_HEADS, HEAD_DIM), 0.1),
        "rwkv_ln_w": 1.0 + nrm((L, RWKV_W), 0.02),
        "rwkv_ln_b": nrm((L, RWKV_W), 0.02),
        "w_out": nrm((L, D_MIX, D), D_MIX ** -0.5),
        "norm_ffn": 1.0 + nrm((L, D), 0.02),
        "w_up": nrm((L, D, 2 * D_FF), D ** -0.5),
        "conv_w": nrm((L, CONV_W, 2 * D_FF), CONV_W ** -0.5),
        "conv_b": nrm((L, 2 * D_FF), 0.02),
        "w_down": nrm((L, D_FF, D), D_FF ** -0.5),
        "norm_final": 1.0 + nrm((D,), 0.02),
    }


def reference(x, c, w_mod, b_mod, norm_mix, w_in, fox_f_bias, rwkv_mu, rwkv_w0, rwkv_w2, rwkv_a0, rwkv_a2,
              rwkv_g2, rwkv_k_k, rwkv_k_a, rwkv_r_k, rwkv_ln_w, rwkv_ln_b, w_out, norm_ffn, w_up, conv_w,
              conv_b, w_down, norm_final):
    c_act = jax.nn.silu(c)
    for l in range(DEPTH):
        mod = c_act @ w_mod[l] + b_mod[l]
        sh1, sc1, g1, sh2, sc2, g2 = (m[:, None, :] for m in jnp.split(mod, 6, axis=-1))
        h = _rmsnorm(x, norm_mix[l]) * (1.0 + sc1) + sh1
        p = h @ w_in[l]
        p_fox, p_rwkv, p_moba = jnp.split(p, [FOX_COLS, FOX_COLS + RWKV_COLS], axis=-1)
        fq, fk, fv, ff = jnp.split(p_fox, [FOX_W, 2 * FOX_W, 3 * FOX_W], axis=-1)
        y_fox = fox_attention(fq, fk, fv, ff, fox_f_bias[l])
        y_rwkv = rwkv7_time_mix(p_rwkv, rwkv_mu[l], rwkv_w0[l], rwkv_w2[l], rwkv_a0[l], rwkv_a2[l], rwkv_g2[l],
                                rwkv_k_k[l], rwkv_k_a[l], rwkv_r_k[l], rwkv_ln_w[l], rwkv_ln_b[l])
        mq, mk, mv = jnp.split(p_moba, [MOBA_W, 2 * MOBA_W], axis=-1)
        y_moba = moba_attention(mq, mk, mv)
        y = jnp.concatenate([y_fox.astype(x.dtype), y_rwkv.astype(x.dtype), y_moba.astype(x.dtype)], axis=-1)
        x = x + g1 * (y @ w_out[l])
        h = _rmsnorm(x, norm_ffn[l]) * (1.0 + sc2) + sh2
        u = _causal_dwconv(h @ w_up[l], conv_w[l], conv_b[l])
        u_gate, u_val = jnp.split(u, 2, axis=-1)
        x = x + g2 * ((jax.nn.silu(u_gate) * u_val) @ w_down[l])
    return _rmsnorm(x, norm_final)
```

```python
import contextlib
import numpy as np
import ml_dtypes
import concourse.bass as bass
import concourse.mybir as mybir
from concourse.bass_utils import run_bass_kernel_spmd

F32 = mybir.dt.float32
BF16 = mybir.dt.bfloat16
AF = mybir.ActivationFunctionType
ALU = mybir.AluOpType
AX = mybir.AxisListType

D = 1024
S = 2048
DEPTH = 2
DIN = 3332
DFF = 2816
NEG = -30000.0
EPOCH = 16000


class T:
    __slots__ = ("name", "w", "r")

    def __init__(self, name):
        self.name = name
        self.w = None
        self.r = []


class Prog:
    ENG = ("pe", "act", "dve", "pool", "sp")

    def __init__(self, nc, stack):
        self.nc = nc
        self.q = {e: [] for e in self.ENG}
        self.cnt = {e: 0 for e in self.ENG}
        self.seen = {e: {} for e in self.ENG}
        self.esems = {e: [] for e in self.ENG}
        self.stack = stack
        self.dma_pool = [stack.enter_context(nc.semaphore("dq%d" % i)) for i in range(48)]
        self.dma_cnt = [0] * len(self.dma_pool)
        self.dma_next = {"hw": 0, "sw": 32}
        self.dma_rng = {"hw": (0, 32), "sw": (32, 48)}
        self.dma_last = {}
        self.nsem = 0
        self.capture = None

    def _esem(self, eng, epoch):
        lst = self.esems[eng]
        while len(lst) <= epoch:
            lst.append(self.stack.enter_context(self.nc.semaphore("e_%s_%d" % (eng, len(lst)))))
        return lst[epoch]

    def _need(self, eng, stamp, waits, raw=True, isdma=False):
        if stamp is None:
            return
        sem, val, seng = stamp
        if seng == eng and (eng == "pe" or not raw) and not isdma:
            return
        k = id(sem)
        if self.seen[eng].get(k, 0) >= val:
            return
        self.seen[eng][k] = val
        waits.append((sem, val))

    def _deps(self, eng, reads, writes, isdma=False):
        waits = []
        for t in reads:
            self._need(eng, t.w, waits, isdma=isdma)
        for t in writes:
            self._need(eng, t.w, waits, raw=False, isdma=isdma)
            for st in t.r:
                self._need(eng, st, waits, raw=False, isdma=isdma)
        return waits

    def op(self, eng, fn, reads=(), writes=()):
        if self.capture is not None:
            self.capture.append(("op", eng, fn, tuple(reads), tuple(writes), None))
            return None
        waits = self._deps(eng, reads, writes)
        idx = self.cnt[eng]
        self.cnt[eng] += 1
        sem = self._esem(eng, idx // EPOCH)
        val = idx % EPOCH + 1
        stamp = (sem, val, eng)
        self.q[eng].append((waits, fn, sem, 1))
        for t in reads:
            t.r.append(stamp)
        for t in writes:
            t.w = stamp
            t.r = []
        return stamp

    def dma(self, eng, out, in_, reads=(), writes=(), **kw):
        if self.capture is not None:
            self.capture.append(("dma", eng, (out, in_), tuple(reads), tuple(writes), kw))
            return None
        waits = self._deps(eng, reads, writes, isdma=True)
        cls = "sw" if eng == "pool" else "hw"
        lo, hi = self.dma_rng[cls]
        i = self.dma_next[cls]
        self.dma_next[cls] = lo + (i + 1 - lo) % (hi - lo)
        sem = self.dma_pool[i]
        prev = self.dma_last.get(i)
        if prev is not None:
            self._need(eng, prev, waits)
        self.dma_cnt[i] += 16
        stamp = (sem, self.dma_cnt[i], 'dma')
        self.dma_last[i] = stamp
        self.q[eng].append((waits, lambda e: e.dma_start(out=out, in_=in_, **kw), sem, 16))
        for t in reads:
            t.r.append(stamp)
        for t in writes:
            t.w = stamp
            t.r = []
        return stamp

    def begin_capture(self):
        self.capture = []

    def end_capture(self):
        lst = self.capture
        self.capture = None
        return lst

    def _play1(self, it):
        kind, eng, a, reads, writes, kw = it
        if kind == "op":
            self.op(eng, a, reads, writes)
        else:
            self.dma(eng, a[0], a[1], reads, writes, **kw)

    def play(self, la, lb=()):
        na, nb = len(la), len(lb)
        ia = ib = 0
        while ia < na or ib < nb:
            if ib >= nb or (ia < na and ia * nb <= ib * na):
                self._play1(la[ia])
                ia += 1
            else:
                self._play1(lb[ib])
                ib += 1

    def barrier(self):
        stamps = []
        for e in self.ENG:
            if self.cnt[e] > 0:
                idx = self.cnt[e] - 1
                stamps.append((self._esem(e, idx // EPOCH), idx % EPOCH + 1, 'bar'))
        for i, st in self.dma_last.items():
            stamps.append(st)
        for e in self.ENG:
            waits = []
            for st in stamps:
                self._need(e, st, waits)
            if waits:
                self.q[e].append((waits, None, None, 0))

    def emit(self):
        nc = self.nc
        engs = {"pe": "tensor", "act": "scalar", "dve": "vector", "pool": "gpsimd", "sp": "sync"}
        with nc.Block() as block:
            for ename, attr in engs.items():
                items = self.q[ename]

                def body(e, items=items):
                    for waits, fn, sem, inc in items:
                        for (s, v) in waits:
                            e.wait_ge(s, v)
                        if fn is not None:
                            fn(e).then_inc(sem, inc)

                getattr(block, attr)(body)


def _col(v):
    v = np.asarray(v)
    n = v.shape[-1] // 128
    return np.ascontiguousarray(v.reshape(n, 128).T)


RW0 = 772
MB0 = 2564
CDEC = 0.6065306597126334
SW = 128
CH = 64


def build(cfg):
    nc = bass.Bass("TRN2", target_bir_lowering=False)
    dbg = cfg.get("debug")
    nlayers = cfg.get("nlayers", DEPTH)
    skip = cfg.get("skip", "")

    def din(name, shape, dt=F32):
        return nc.dram_tensor(name, list(shape), dt, kind="ExternalInput").ap()

    xT_d = din("xT", [D, S])
    cT_d = din("cT", [128, 8])
    wmod_d = din("w_mod", [DEPTH, D, 6 * D])
    bmod_d = din("b_modc", [DEPTH, 128, 48])
    nmix_d = din("norm_mixc", [DEPTH, 128, 8])
    nffn_d = din("norm_ffnc", [DEPTH, 128, 8])
    nfin_d = din("norm_finc", [128, 8])
    win_d = din("w_in", [DEPTH, D, DIN])
    wout_d = din("w_out", [DEPTH, D, D])
    wup_d = din("w_up", [DEPTH, D, 2 * DFF])
    wdn_d = din("w_down", [DEPTH, DFF, D])
    cw_d = din("conv_wc", [DEPTH, 128, 3, 44])
    cb_d = din("conv_bc", [DEPTH, 128, 44])
    fb_d = din("fox_fb", [DEPTH, 4, 1])
    r64_d = din("rw64", [DEPTH, 64, 8 * 7 + 24 + 2])
    mug_d = din("rw_mug", [DEPTH, 128, 1])
    w2_d = din("rwkv_w2", [DEPTH, 64, 512])
    a2_d = din("rwkv_a2", [DEPTH, 64, 512])
    g2_d = din("rwkv_g2", [DEPTH, 128, 512])
    c_cb = din("c_cb", [128, 4, 512], BF16)
    c_idb = din("c_idb", [128, 128], BF16)
    c_idf = din("c_idf", [128, 128], F32)
    c_sel = din("c_sel", [64, 4, 4], BF16)
    c_rows = din("c_rows", [12, S], BF16)
    c_oneh = din("c_oneh", [8, S], BF16)
    c_negm = din("c_negm", [128, 16, 8], F32)
    c_past = din("c_past", [128, 16, 8], F32)
    c_mL = din("c_mL", [64, 8, 64], F32)
    c_mU = din("c_mU", [64, 8, 64], F32)
    c_mUi = din("c_mUi", [64, 8, 64], F32)
    c_rst = din("c_rst", [64, 8, SW], F32)

    out_d = nc.dram_tensor("outT", [D, S], F32, kind="ExternalOutput").ap()
    pT_d = nc.dram_tensor("pT", [DIN, S], F32).ap()
    yT_d = nc.dram_tensor("yT", [D, S], BF16).ap()
    zT_d = nc.dram_tensor("zT", [DFF, S], BF16).ap()
    xs_d = nc.dram_tensor("xs", [7, 64, S], F32).ap()
    dbg_d = None
    if dbg in ("yT",):
        dbg_d = nc.dram_tensor("dbg", [D, S], BF16, kind="ExternalOutput").ap()
    if dbg == "rw":
        dbg_d = nc.dram_tensor("dbg", [12, 64, 8, SW], F32, kind="ExternalOutput").ap()
        dbg2_d = nc.dram_tensor("dbg2", [16, 64, 8, 64], F32, kind="ExternalOutput").ap()
    if dbg in ("xm", "xf"):
        dbg_d = nc.dram_tensor("dbg", [D, S], F32, kind="ExternalOutput").ap()

    with contextlib.ExitStack() as stack:
        P = Prog(nc, stack)

        uid = [0]

        def sb(name, shape, dt=F32, st=stack):
            uid[0] += 1
            return st.enter_context(nc.sbuf_tensor("s%d_%s" % (uid[0], name), list(shape), dt))

        def ps(name, shape, dt=F32, st=stack):
            uid[0] += 1
            return st.enter_context(nc.psum_tensor("p%d_%s" % (uid[0], name), list(shape), dt))

        dq = ["sp", "act", "pool"]

        xT = sb("xT", [128, 8, S])
        xT_t = [[T("xT%d_%d" % (j, c)) for c in range(4)] for j in range(8)]
        ones_bf = sb("ones_bf", [128, 128], BF16)
        ones_f = sb("ones_f", [64, 64], F32)
        mean_f = sb("mean_f", [64, 64], F32)
        idb = sb("idb", [128, 128], BF16)
        idf = sb("idf", [128, 128], F32)
        cbias = sb("cbias", [128, 4, 512], BF16)
        ones_t = T("ones")
        eps_col = sb("eps_col", [128, 8])
        eps_t = T("eps")
        cact = sb("cact", [128, 8])
        cact_t = T("cact")
        modT = sb("modT", [128, 48])
        mod_t = T("modT")
        gm = sb("gm", [128, 16])
        gm_t = T("gm")
        bmod = sb("bmod", [128, 48])
        bmod_t = T("bmod")
        nmix = sb("nmix", [128, 16])
        nmix_t = T("nmix")
        nfin = sb("nfin", [128, 8])
        nfin_t = T("nfin")

        P.op("pool", lambda e: e.memset(ones_bf[:], 1.0), writes=[ones_t])
        P.op("pool", lambda e: e.memset(ones_f[:], 1.0), writes=[ones_t])
        P.op("pool", lambda e: e.memset(mean_f[:], 1.0 / 64.0), writes=[ones_t])
        P.op("pool", lambda e: e.memset(eps_col[:, 0:1], 1e-6), writes=[eps_t])
        P.op("pool", lambda e: e.memset(eps_col[:, 1:2], 64e-5), writes=[eps_t])
        P.op("pool", lambda e: e.memset(eps_col[:, 2:3], 1.0), writes=[eps_t])
        P.op("pool", lambda e: e.memset(eps_col[:, 3:4], 0.0), writes=[eps_t])
        P.op("pool", lambda e: e.memset(eps_col[:, 4:5], 1e-18), writes=[eps_t])
        P.dma("pool", idb[:], c_idb[:, :], writes=[ones_t])
        P.dma("pool", idf[:], c_idf[:, :], writes=[ones_t])
        P.dma("pool", cbias[:], c_cb[:, :, :], writes=[ones_t])
        P.dma("pool", nfin[:], nfin_d[:, :], writes=[nfin_t])
        for j in range(8):
            P.dma(dq[j % 2], xT[:, j, :], xT_d[j * 128:(j + 1) * 128, :], writes=xT_t[j])
        P.dma("pool", cact[:], cT_d[:, :], writes=[cact_t])
        P.op("act", lambda e: e.activation(out=cact[:], in_=cact[:], func=AF.Silu),
             reads=[cact_t], writes=[cact_t])
        P.barrier()

        def norm_stage(gcol, shcol, par_t, sink):
            with contextlib.ExitStack() as st1:
                sq = [sb("sq%d" % i, [128, 512], BF16, st1) for i in range(3)]
                sq_t = [T("sq%d" % i) for i in range(3)]
                pss = [ps("pss%d" % i, [128, 512], F32, st1) for i in range(2)]
                pss_t = [T("pss%d" % i) for i in range(2)]
                rstd = [sb("rstd%d" % i, [128, 512], F32, st1) for i in range(2)]
                rstd_t = [T("rstd%d" % i) for i in range(2)]
                tmp = [sb("ntmp%d" % i, [128, 512], F32, st1) for i in range(3)]
                tmp_t = [T("ntmp%d" % i) for i in range(3)]
                n = 0
                for c in range(4):
                    cs = slice(c * 512, (c + 1) * 512)
                    pb = c % 2
                    for j in range(8):
                        i = n % 3
                        n += 1
                        P.op("act", lambda e, i=i, j=j, cs=cs: e.activation(
                            out=sq[i][:], in_=xT[:, j, cs], func=AF.Square),
                            reads=[xT_t[j][c]], writes=[sq_t[i]])
                        P.op("pe", lambda e, i=i, j=j, pb=pb: e.matmul(
                            pss[pb][:], lhsT=ones_bf[:], rhs=sq[i][:], start=(j == 0), stop=(j == 7)),
                            reads=[sq_t[i], ones_t], writes=[pss_t[pb]])
                    P.op("act", lambda e, pb=pb: e.activation(
                        out=rstd[pb][:], in_=pss[pb][:], func=AF.Ln, bias=eps_col[:, 0:1], scale=1.0 / D),
                        reads=[pss_t[pb], eps_t], writes=[rstd_t[pb]])
                    P.op("act", lambda e, pb=pb: e.activation(out=rstd[pb][:], in_=rstd[pb][:], func=AF.Exp, scale=-0.5),
                         reads=[rstd_t[pb]], writes=[rstd_t[pb]])
                    for j in range(8):
                        i = n % 3
                        n += 1
                        P.op("dve", lambda e, i=i, j=j, cs=cs, pb=pb: e.tensor_tensor(
                            out=tmp[i][:], in0=xT[:, j, cs], in1=rstd[pb][:], op=ALU.mult),
                            reads=[xT_t[j][c], rstd_t[pb]], writes=[tmp_t[i]])
                        sink(j, c, cs, tmp[i], tmp_t[i], gcol(j), shcol(j), par_t)
                P.barrier()

        def layer_body(l):
            with contextlib.ExitStack() as st0:
                wm = [sb("wm%d" % i, [128, 8, 512], F32, st0) for i in range(2)]
                wm_t = [T("wm%d" % i) for i in range(2)]
                ps_mod_full = ps("ps_mod", [128, 512], F32, st0)
                ps_mod = ps_mod_full[:, 0:48]
                psm_t = T("ps_mod")
                P.dma("pool", bmod[:], bmod_d[l], writes=[bmod_t])
                P.dma("pool", nmix[:, 0:8], nmix_d[l], writes=[nmix_t])
                P.dma("pool", nmix[:, 8:16], nffn_d[l], writes=[nmix_t])
                wv = wmod_d[l].rearrange("(k p) n -> p k n", p=128)
                for nb in range(12):
                    b = nb % 2
                    P.dma(dq[nb % 2], wm[b][:], wv[:, :, nb * 512:(nb + 1) * 512], writes=[wm_t[b]])
                    for jj in range(4):
                        j = nb * 4 + jj
                        for k in range(8):
                            P.op("pe", lambda e, b=b, jj=jj, j=j, k=k: e.matmul(
                                ps_mod[:, j:j + 1], lhsT=wm[b][:, k, jj * 128:(jj + 1) * 128],
                                rhs=cact[:, k:k + 1], start=(k == 0), stop=(k == 7)),
                                reads=[wm_t[b], cact_t], writes=[psm_t])
                P.op("dve", lambda e: e.tensor_tensor(out=modT[:], in0=ps_mod, in1=bmod[:], op=ALU.add),
                     reads=[psm_t, bmod_t], writes=[mod_t])
                P.op("dve", lambda e: e.scalar_tensor_tensor(
                    out=gm[:, 0:8], in0=modT[:, 8:16], scalar=1.0, in1=nmix[:, 0:8], op0=ALU.add, op1=ALU.mult),
                    reads=[mod_t, nmix_t], writes=[gm_t])
                P.op("dve", lambda e: e.scalar_tensor_tensor(
                    out=gm[:, 8:16], in0=modT[:, 32:40], scalar=1.0, in1=nmix[:, 8:16], op0=ALU.add, op1=ALU.mult),
                    reads=[mod_t, nmix_t], writes=[gm_t])
                P.barrier()

            with contextlib.ExitStack() as stL:
                vtm = [sb("vtm%d" % i, [128, 16, 256], BF16, stL) for i in range(2)]
                vtm_t = [T("vtm%d" % i) for i in range(2)]

                with contextlib.ExitStack() as stH:
                    hT = sb("hT", [128, 8, S], BF16, stH)
                    hT_t = [[T("hT%d_%d" % (j, c)) for c in range(4)] for j in range(8)]

                    def sink_h(j, c, cs, tp, tp_t, g, sh, par_t):
                        P.op("act", lambda e: e.activation(out=hT[:, j, cs], in_=tp[:], func=AF.Identity,
                                                           bias=sh, scale=g),
                             reads=[tp_t] + par_t, writes=[hT_t[j][c]])

                    norm_stage(lambda j: gm[:, j:j + 1], lambda j: modT[:, j:j + 1], [gm_t, mod_t], sink_h)

                    with contextlib.ExitStack() as st2:
                        wst = [sb("wst%d" % i, [128, 8, 256], F32, st2) for i in range(2)]
                        wst_t = [T("wst%d" % i) for i in range(2)]
                        wbf = [sb("wbf%d" % i, [128, 8, 256], BF16, st2) for i in range(2)]
                        wbf_t = [T("wbf%d" % i) for i in range(2)]
                        pp = [ps("pp%d" % i, [128, S], F32, st2) for i in range(2)]
                        pp_t = [[T("pp%d_%d" % (i, c)) for c in range(4)] for i in range(2)]
                        ob = [sb("ob%d" % i, [128, S], F32, st2) for i in range(2)]
                        ob_t = [T("ob%d" % i) for i in range(2)]
                        wiv = win_d[l].rearrange("(k p) n -> p k n", p=128)
                        segs = [(0, 512), (768, 4), (RW0, 1792), (MB0, 512)]
                        tiles = []
                        for (s0, ln) in segs:
                            o = 0
                            while o < ln:
                                m = min(128, ln - o)
                                tiles.append((s0 + o, m))
                                o += m
                        it = 0
                        for (n0, m) in tiles:
                            b = it % 2
                            it += 1
                            P.dma("sp", wst[b][:, :, 0:m], wiv[:, :, n0:n0 + m], writes=[wst_t[b]])
                            P.op("pool", lambda e, b=b, m=m: e.tensor_copy(out=wbf[b][:, :, 0:m], in_=wst[b][:, :, 0:m]),
                                 reads=[wst_t[b]], writes=[wbf_t[b]])
                            for c in range(4):
                                cs = slice(c * 512, (c + 1) * 512)
                                for k in range(8):
                                    P.op("pe", lambda e, b=b, m=m, k=k, cs=cs: e.matmul(
                                        pp[b][0:m, cs], lhsT=wbf[b][:, k, 0:m], rhs=hT[:, k, cs],
                                        start=(k == 0), stop=(k == 7)),
                                        reads=[wbf_t[b], hT_t[k][c]], writes=[pp_t[b][c]])
                                if c % 2 == 0:
                                    P.op("act", lambda e, b=b, m=m, cs=cs: e.activation(
                                        out=ob[b][0:m, cs], in_=pp[b][0:m, cs], func=AF.Identity),
                                        reads=[pp_t[b][c]], writes=[ob_t[b]])
                                else:
                                    P.op("dve", lambda e, b=b, m=m, cs=cs: e.tensor_copy(
                                        out=ob[b][0:m, cs], in_=pp[b][0:m, cs]),
                                        reads=[pp_t[b][c]], writes=[ob_t[b]])
                            P.dma("act", pT_d[n0:n0 + m, :], ob[b][0:m, :], reads=[ob_t[b]])
                        for vi, v0 in enumerate((512, MB0 + 512)):
                            b = it % 2
                            it += 1
                            P.dma("sp", wst[b][:, :, 0:256], wiv[:, :, v0:v0 + 256], writes=[wst_t[b]])
                            P.op("pool", lambda e, b=b: e.tensor_copy(out=wbf[b][:, :, :], in_=wst[b][:, :, :]),
                                 reads=[wst_t[b]], writes=[wbf_t[b]])
                            for tt in range(16):
                                pi = tt % 2
                                c = tt // 4
                                for k in range(8):
                                    P.op("pe", lambda e, b=b, k=k, tt=tt, pi=pi: e.matmul(
                                        pp[pi][:, 0:256], lhsT=hT[:, k, tt * 128:(tt + 1) * 128], rhs=wbf[b][:, k, :],
                                        start=(k == 0), stop=(k == 7)),
                                        reads=[wbf_t[b], hT_t[k][c]], writes=[pp_t[pi][0]])
                                P.op("act" if tt % 2 == 0 else "dve",
                                     (lambda e, vi=vi, tt=tt, pi=pi: e.activation(
                                         out=vtm[vi][:, tt, :], in_=pp[pi][:, 0:256], func=AF.Identity)) if tt % 2 == 0 else
                                     (lambda e, vi=vi, tt=tt, pi=pi: e.tensor_copy(out=vtm[vi][:, tt, :], in_=pp[pi][:, 0:256])),
                                     reads=[pp_t[pi][0]], writes=[vtm_t[vi]])
                        P.barrier()

                def attention(kind):
                    fox = (kind == "fox")
                    KA = 71 if fox else 73
                    qrow0 = 0 if fox else MB0
                    krow0 = 256 if fox else MB0 + 256
                    yrow0 = 0 if fox else 768
                    vt, vt_t = (vtm[0], vtm_t[0]) if fox else (vtm[1], vtm_t[1])
                    with contextlib.ExitStack() as sa:
                        Qh = [sb("Qh%d" % h, [80, S], BF16, sa) for h in range(4)]
                        Kh = [sb("Kh%d" % h, [80, S], BF16, sa) for h in range(4)]
                        Qh_t = [T("Qh%d" % h) for h in range(4)]
                        Kh_t = [T("Kh%d" % h) for h in range(4)]
                        with contextlib.ExitStack() as sp_:
                            stg = [sb("stg%d" % i, [64, S], F32, sp_) for i in range(2)]
                            stg_t = [T("stg%d" % i) for i in range(2)]
                            sqq = [sb("sqq%d" % i, [64, S], BF16, sp_) for i in range(2)]
                            sqq_t = [T("sqq%d" % i) for i in range(2)]
                            selb = sb("selb", [64, 4, 4], BF16, sp_)
                            sel_t = T("selb")
                            P.dma("pool", selb[:], c_sel[:, :, :], writes=[sel_t])
                            fA = sb("fA", [4, S], F32, sp_)
                            fA_t = T("fA")
                            mrow = sb("mrow", [4, S], BF16, sp_)
                            mrow_t = T("mrow")
                            kmax = sb("kmax", [4, 2], F32, sp_)
                            kmax_t = T("kmax")
                            psn = ps("psn", [4, S], F32, sp_)
                            psn_t = T("psn")
                            if fox:
                                fB = sb("fB", [4, S], F32, sp_)
                                fB_t = T("fB")
                                g3 = sb("g3", [4, 3, S], BF16, sp_)
                                g3_t = T("g3")
                                fbb = sb("fbb", [4, 2], F32, sp_)
                                fbb_t = T("fbb")
                            else:
                                kmT = sb("kmT", [64, 4, 8], F32, sp_)
                                kmT_t = T("kmT")
                                psg_full = ps("psg", [128, 64, 8], F32, sp_)
                                psg = psg_full[:, 0:16, :]
                                psg_t = T("psg")
                                psT = [ps("psT%d" % i, [8, 512], F32, sp_) for i in range(2)]
                                psT_t = [T("psT%d" % i) for i in range(2)]
                                gmk = sb("gmk", [128, 16, 8], F32, sp_)
                                gmk_t = T("gmk")
                                top8 = sb("top8", [128, 16, 8], F32, sp_)
                                top8_t = T("top8")
                                negm = sb("negm", [128, 16, 8], F32, sp_)
                                past = sb("past", [128, 16, 8], F32, sp_)
                                cm_t = T("cmask")
                                P.dma("pool", negm[:], c_negm[:, :, :], writes=[cm_t])
                                P.dma("pool", past[:], c_past[:, :, :], writes=[cm_t])
                                mbT = [sb("mbT%d" % i, [8, S], BF16, sp_) for i in range(2)]
                                mbT_t = [T("mbT%d" % i) for i in range(2)]

                            for h in range(4):
                                b = h % 2
                                P.dma(dq[h % 2], stg[b][:], pT_d[krow0 + h * 64: krow0 + (h + 1) * 64, :], writes=[stg_t[b]])
                                P.op("dve", lambda e, h=h, b=b: e.tensor_copy(out=Kh[h][0:64, :], in_=stg[b][:]),
                                     reads=[stg_t[b]], writes=[Kh_t[h]])
                                P.op("act", lambda e, b=b: e.activation(out=sqq[b][:], in_=stg[b][:], func=AF.Square),
                                     reads=[stg_t[b]], writes=[sqq_t[b]])
                                for c in range(4):
                                    cs = slice(c * 512, (c + 1) * 512)
                                    P.op("pe", lambda e, h=h, b=b, cs=cs: e.matmul(
                                        psn[:, cs], lhsT=selb[:, h, :], rhs=sqq[b][:, cs], start=(h == 0), stop=(h == 3)),
                                        reads=[sqq_t[b], sel_t], writes=[psn_t])
                                if not fox:
                                    P.op("dve", lambda e, h=h, b=b: e.tensor_reduce(
                                        out=kmT[:, h, :], in_=stg[b][:].rearrange("p (n s) -> p n s", s=256),
                                        axis=AX.X, op=ALU.add), reads=[stg_t[b]], writes=[kmT_t])
                            if not fox:
                                P.op("dve", lambda e: e.tensor_scalar(out=kmT[:], in0=kmT[:], scalar1=1.0 / 256.0, scalar2=None,
                                                                      op0=ALU.mult), reads=[kmT_t], writes=[kmT_t])
                            P.op("dve", lambda e: e.tensor_reduce(out=kmax[:, 0:1], in_=psn[:, :], axis=AX.X, op=ALU.max),
                                 reads=[psn_t], writes=[kmax_t])
                            P.op("dve", lambda e: e.tensor_scalar(out=kmax[:, 1:2], in0=kmax[:, 0:1], scalar1=1.0 / 64.0,
                                                                  scalar2=None, op0=ALU.mult),
                                 reads=[kmax_t], writes=[kmax_t])
                            for h in range(4):
                                b = h % 2
                                P.dma(dq[h % 2], stg[b][:], pT_d[qrow0 + h * 64: qrow0 + (h + 1) * 64, :], writes=[stg_t[b]])
                                P.op("dve", lambda e, h=h, b=b: e.tensor_scalar(
                                    out=Qh[h][0:64, :], in0=stg[b][:], scalar1=0.125, scalar2=None, op0=ALU.mult),
                                    reads=[stg_t[b]], writes=[Qh_t[h]])
                                P.op("act", lambda e, b=b: e.activation(out=sqq[b][:], in_=stg[b][:], func=AF.Square),
                                     reads=[stg_t[b]], writes=[sqq_t[b]])
                                for c in range(4):
                                    cs = slice(c * 512, (c + 1) * 512)
                                    P.op("pe", lambda e, h=h, b=b, cs=cs: e.matmul(
                                        psn[:, cs], lhsT=selb[:, h, :], rhs=sqq[b][:, cs], start=(h == 0), stop=(h == 3)),
                                        reads=[sqq_t[b], sel_t, kmax_t], writes=[psn_t])
                                if not fox:
                                    for tt in range(16):
                                        P.op("pe", lambda e, h=h, b=b, tt=tt: e.matmul(
                                            psg[:, tt, :], lhsT=stg[b][:, tt * 128:(tt + 1) * 128], rhs=kmT[:, h, :],
                                            start=True, stop=True), reads=[stg_t[b], kmT_t], writes=[psg_t])
                                    P.op("dve", lambda e: e.tensor_tensor(out=gmk[:], in0=psg, in1=negm[:], op=ALU.add),
                                         reads=[psg_t, cm_t], writes=[gmk_t])
                                    for tt in range(16):
                                        P.op("dve", lambda e, tt=tt: e.max(out=top8[:, tt, :], in_=gmk[:, tt, :]),
                                             reads=[gmk_t], writes=[top8_t])
                                    P.op("dve", lambda e: e.tensor_tensor(
                                        out=gmk[:], in0=gmk[:], in1=top8[:, :, 2:3].to_broadcast([128, 16, 8]), op=ALU.is_ge),
                                        reads=[gmk_t, top8_t], writes=[gmk_t])
                                    P.op("dve", lambda e: e.tensor_scalar(
                                        out=gmk[:], in0=gmk[:], scalar1=-1.0, scalar2=-NEG, op0=ALU.add, op1=ALU.mult),
                                        reads=[gmk_t], writes=[gmk_t])
                                    P.op("dve", lambda e: e.tensor_tensor(out=gmk[:], in0=gmk[:], in1=past[:], op=ALU.mult),
                                         reads=[gmk_t, cm_t], writes=[gmk_t])
                                    mb_ = h % 2
                                    for tt in range(16):
                                        pi = (tt // 4) % 2
                                        P.op("pe", lambda e, tt=tt, pi=pi: e.transpose(
                                            psT[pi][:, (tt % 4) * 128:(tt % 4 + 1) * 128], gmk[:, tt, :], idf[:, :]),
                                            reads=[gmk_t, ones_t], writes=[psT_t[pi]])
                                        if tt % 4 == 3:
                                            c = tt // 4
                                            P.op("act", lambda e, pi=pi, c=c, mb_=mb_: e.activation(
                                                out=mbT[mb_][:, c * 512:(c + 1) * 512], in_=psT[pi][:, :], func=AF.Identity),
                                                reads=[psT_t[pi]], writes=[mbT_t[mb_]])
                                    P.dma("sp", Qh[h][64:72, :], mbT[mb_][:, :], reads=[mbT_t[mb_]], writes=[Qh_t[h]])
                            P.op("act", lambda e: e.activation(out=fA[:], in_=psn[:, :], func=AF.Sqrt, scale=kmax[:, 1:2]),
                                 reads=[psn_t, kmax_t], writes=[fA_t])
                            P.op("dve", lambda e: e.tensor_scalar(out=mrow[:], in0=fA[:], scalar1=-1.0, scalar2=None,
                                                                  op0=ALU.mult), reads=[fA_t], writes=[mrow_t])
                            mr = 70 if fox else 72
                            for h in range(4):
                                P.dma(dq[h % 3], Qh[h][mr:mr + 1, :], mrow[h:h + 1, :], reads=[mrow_t], writes=[Qh_t[h]])
                                P.dma(dq[(h + 1) % 3], Kh[h][mr:mr + 1, :], c_rows[6:7, :], writes=[Kh_t[h]])
                            if fox:
                                P.dma("sp", fA[:], pT_d[768:772, :], writes=[fA_t])
                                P.dma("pool", fbb[:, 0:1], fb_d[l], writes=[fbb_t])
                                P.op("dve", lambda e: e.tensor_scalar(out=fbb[:, 1:2], in0=fbb[:, 0:1], scalar1=-1.0,
                                                                      scalar2=None, op0=ALU.mult),
                                     reads=[fbb_t], writes=[fbb_t])
                                P.op("act", lambda e: e.activation(out=fA[:], in_=fA[:], func=AF.Exp, bias=fbb[:, 1:2], scale=-1.0),
                                     reads=[fA_t, fbb_t], writes=[fA_t])
                                P.op("act", lambda e: e.activation(out=fA[:], in_=fA[:], func=AF.Ln, bias=eps_col[0:4, 2:3], scale=1.0),
                                     reads=[fA_t, eps_t], writes=[fA_t])
                                P.op("dve", lambda e: e.tensor_tensor_scan(
                                    out=fB[:], data0=eps_col[0:4, 2:3].to_broadcast([4, S]), data1=fA[:], initial=0.0,
                                    op0=ALU.mult, op1=ALU.add), reads=[fA_t, eps_t], writes=[fB_t])
                                for i in range(3):
                                    P.op("dve", lambda e, i=i: e.tensor_copy(out=g3[:, i, :], in_=fB[:]),
                                         reads=[fB_t], writes=[g3_t])
                                    if i < 2:
                                        P.op("dve", lambda e, i=i: e.tensor_tensor(out=fB[:], in0=fB[:], in1=g3[:, i, :],
                                                                                   op=ALU.subtract),
                                             reads=[fB_t, g3_t], writes=[fB_t])
                                n = 0
                                for h in range(4):
                                    for i in range(3):
                                        P.dma(dq[n % 3], Qh[h][64 + i:65 + i, :], g3[h:h + 1, i, :], reads=[g3_t], writes=[Qh_t[h]])
                                        P.dma(dq[(n + 1) % 3], Kh[h][67 + i:68 + i, :], g3[h:h + 1, i, :], reads=[g3_t], writes=[Kh_t[h]])
                                        n += 1
                                    P.dma(dq[n % 3], Qh[h][67:70, :], c_rows[0:3, :], writes=[Qh_t[h]])
                                    P.dma(dq[(n + 1) % 3], Kh[h][64:67, :], c_rows[3:6, :], writes=[Kh_t[h]])
                            else:
                                for h in range(4):
                                    P.dma(dq[h % 3], Kh[h][64:72, :], c_oneh[:, :], writes=[Kh_t[h]])
                            P.barrier()

                        with contextlib.ExitStack() as sm:
                            pss = [ps("as%d" % i, [128, 512], F32, sm) for i in range(3)]
                            pss_t = [T("as%d" % i) for i in range(3)]
                            pt = [sb("apt%d" % i, [128, 512], BF16, sm) for i in range(3)]
                            pt_t = [T("apt%d" % i) for i in range(3)]
                            psy = [ps("ay%d" % i, [128, 512], F32, sm) for i in range(2)]
                            psd = [ps("ad%d" % i, [128, 512], F32, sm) for i in range(2)]
                            psy_t = [T("ay%d" % i) for i in range(2)]
                            psd_t = [T("ad%d" % i) for i in range(2)]
                            rec = [sb("arec%d" % i, [128, 512], F32, sm) for i in range(2)]
                            rec_t = [T("arec%d" % i) for i in range(2)]
                            ysb = [sb("aysb%d" % i, [128, 512], F32, sm) for i in range(2)]
                            ysb_t = [T("aysb%d" % i) for i in range(2)]
                            yo = [sb("ayo%d" % i, [128, 512], BF16, sm) for i in range(2)]
                            yo_t = [T("ayo%d" % i) for i in range(2)]
                            items = []
                            grp = 0
                            for pair in range(2):
                                for qc in range(4):
                                    nk = 4 * qc + 4
                                    for hh in range(2):
                                        for kt in range(nk):
                                            items.append((grp, pair, qc, hh, kt, nk, hh == 1 and kt == nk - 1))
                                    grp += 1

                            def emit_qk(idx):
                                grp, pair, qc, hh, kt, nk, lastg = items[idx]
                                i = idx % 3
                                h = 2 * pair + hh
                                qs = slice(qc * 512, (qc + 1) * 512)
                                diag = kt >= 4 * qc
                                P.op("pe", lambda e: e.matmul(
                                    pss[i][:], lhsT=Kh[h][0:KA, kt * 128:(kt + 1) * 128], rhs=Qh[h][0:KA, qs],
                                    start=True, stop=(not diag)),
                                    reads=[Kh_t[h], Qh_t[h]], writes=[pss_t[i]])
                                if diag:
                                    j = kt - 4 * qc
                                    P.op("pe", lambda e: e.matmul(
                                        pss[i][:], lhsT=idb[:, :], rhs=cbias[:, j, :], start=False, stop=True),
                                        reads=[ones_t], writes=[pss_t[i]])

                            def emit_rest(idx):
                                grp, pair, qc, hh, kt, nk, lastg = items[idx]
                                i = idx % 3
                                o = grp % 2
                                h = 2 * pair + hh
                                pb = 64 * hh
                                qs = slice(qc * 512, (qc + 1) * 512)
                                P.op("act", lambda e: e.activation(out=pt[i][:], in_=pss[i][:], func=AF.Exp),
                                     reads=[pss_t[i]], writes=[pt_t[i]])
                                if idx + 2 < len(items):
                                    emit_qk(idx + 2)
                                P.op("pe", lambda e: e.matmul(
                                    psy[o][pb:pb + 64, :], lhsT=vt[:, kt, h * 64:(h + 1) * 64], rhs=pt[i][:],
                                    start=(kt == 0), stop=(kt == nk - 1)),
                                    reads=[pt_t[i], vt_t], writes=[psy_t[o]])
                                P.op("pe", lambda e: e.matmul(
                                    psd[o][pb:pb + 64, :], lhsT=ones_bf[:, 0:64], rhs=pt[i][:],
                                    start=(kt == 0), stop=(kt == nk - 1)),
                                    reads=[pt_t[i], ones_t], writes=[psd_t[o]])
                                if lastg:
                                    P.op("dve", lambda e: e.reciprocal(out=rec[o][:], in_=psd[o][:]),
                                         reads=[psd_t[o]], writes=[rec_t[o]])
                                    P.op("act", lambda e: e.activation(out=ysb[o][:], in_=psy[o][:], func=AF.Identity),
                                         reads=[psy_t[o]], writes=[ysb_t[o]])
                                    P.op("dve", lambda e: e.tensor_tensor(out=yo[o][:], in0=ysb[o][:], in1=rec[o][:], op=ALU.mult),
                                         reads=[ysb_t[o], rec_t[o]], writes=[yo_t[o]])
                                    r0 = yrow0 + pair * 128
                                    P.dma("sp", yT_d[r0:r0 + 128, qs], yo[o][:], reads=[yo_t[o]])

                            emit_qk(0)
                            emit_qk(1)
                            for idx in range(len(items)):
                                emit_rest(idx)
                            P.barrier()

                if "fox" not in skip:
                    attention("fox")
                if "moba" not in skip:
                    attention("moba")

            def rwkv():
                with contextlib.ExitStack() as sr:
                    prm = sb("rprm", [64, 82], F32, sr)
                    prm_t = T("rprm")
                    w0c, a0c, kkc, kac, rkc, lnw, lnb = (prm[:, i * 8:(i + 1) * 8] for i in range(7))
                    mu_rkv = prm[:, 56:80]
                    mu_w = prm[:, 80:81]
                    mu_a = prm[:, 81:82]
                    omka = sb("omka", [64, 8], F32, sr)
                    mug = sb("rmug", [128, 1], F32, sr)
                    w2s = sb("w2s", [64, 512], F32, sr)
                    a2s = sb("a2s", [64, 512], F32, sr)
                    g2s = sb("g2s", [128, 512], F32, sr)
                    rst = sb("rst", [64, 8, SW], F32, sr)
                    P.dma("sp", prm[:], r64_d[l], writes=[prm_t])
                    P.dma("act", mug[:], mug_d[l], writes=[prm_t])
                    P.dma("sp", w2s[:], w2_d[l], writes=[prm_t])
                    P.dma("act", a2s[:], a2_d[l], writes=[prm_t])
                    P.dma("pool", g2s[:], g2_d[l], writes=[prm_t])
                    P.dma("sp", rst[:], c_rst[:, :, :], writes=[prm_t])
                    P.op("dve", lambda e: e.tensor_scalar(out=omka[:], in0=kac, scalar1=-1.0, scalar2=1.0,
                                                          op0=ALU.mult, op1=ALU.add), reads=[prm_t], writes=[prm_t])
                    ST = sb("ST", [64, 8, 64], F32, sr)
                    ST_t = T("ST")
                    P.op("pool", lambda e: e.memset(ST[:], 0.0), writes=[ST_t])

                    def arr(name, shape=(64, 8, SW), dt=F32):
                        return sb("r_" + name, list(shape), dt, sr), T("r_" + name)

                    rawR, rawR_t = arr("rawR", (64, 8, SW + 1))
                    rawK, rawK_t = arr("rawK", (64, 8, SW + 1))
                    rawV, rawV_t = arr("rawV", (64, 8, SW + 1))
                    rawW, rawW_t = arr("rawW", (64, SW + 1))
                    rawA, rawA_t = arr("rawA", (64, SW + 1))
                    rawG, rawG_t = arr("rawG", (128, SW + 1))
                    TW, TW_t = arr("TW", (64, SW))
                    AL, AL_t = arr("AL", (64, SW))
                    GL, GL_t = arr("GL", (128, SW))
                    SETN = ["R", "K", "V", "KK", "A", "G", "SG", "CS", "D1", "EW", "EWI", "BON"]

                    def mkset(si):
                        Zs = {}
                        for ni, nm in enumerate(SETN):
                            if si == 0:
                                tt_, t_ = arr(nm)
                                Zs[nm] = (tt_[:], t_)
                            else:
                                j_, hf_ = ni // 2, ni % 2
                                ap_ = xT[0:64, j_, hf_ * 1024:(hf_ + 1) * 1024].rearrange("p (h t) -> p h t", t=SW)
                                Zs[nm] = (ap_, T("r1_" + nm))
                        for nm in ("ABb", "BBb", "KBb", "RBb", "Vb"):
                            tt_, t_ = arr("%s%d" % (nm, si), (64, 8, SW), BF16)
                            Zs[nm] = (tt_[:], t_)
                        return Zs
                    sets = [mkset(0), mkset(1)]
                    YR, YR_t = arr("YR")
                    YO, YO_t = arr("YO", (64, 8, SW), BF16)
                    def v16(a):
                        return a[:].rearrange("p h t -> p (h t)").rearrange("p (g i) -> p g i", i=64)
                    TT, TT_t = arr("TT", (64, 16, 64), BF16)
                    Z_, Z_t = arr("Z", (64, 8, 64), BF16)
                    U_, U_t = arr("U", (64, 8, 64), BF16)
                    STb, STb_t = arr("STb", (64, 8, 64), BF16)
                    D1b, D1b_t = arr("D1b", (64, 8, SW), BF16)
                    YRb, YRb_t = arr("YRb", (64, 8, SW), BF16)
                    mean_b = sb("mean_b", [64, 64], BF16, sr)
                    P.op("pool", lambda e: e.memset(mean_b[:], 1.0 / 64.0), writes=[prm_t])
                    w2b = sb("w2b", [64, 512], BF16, sr)
                    a2b = sb("a2b", [64, 512], BF16, sr)
                    g2b = sb("g2b", [128, 512], BF16, sr)
                    P.op("pool", lambda e: e.tensor_copy(out=w2b[:], in_=w2s[:]), reads=[prm_t], writes=[prm_t])
                    P.op("pool", lambda e: e.tensor_copy(out=a2b[:], in_=a2s[:]), reads=[prm_t], writes=[prm_t])
                    P.op("pool", lambda e: e.tensor_copy(out=g2b[:], in_=g2s[:]), reads=[prm_t], writes=[prm_t])
                    TWb, TWb_t = arr("TWb", (64, SW), BF16)
                    ALb, ALb_t = arr("ALb", (64, SW), BF16)
                    GLb, GLb_t = arr("GLb", (128, SW), BF16)
                    Xs = [arr("X%d" % i, (64, 16, 64), BF16) for i in range(2)]
                    Ys = [arr("Y%d" % i, (64, 16, 64), BF16) for i in range(2)]
                    As = [arr("A%d" % i, (64, 16, 64), BF16) for i in range(2)]
                    Bs = [arr("B%d" % i, (64, 16, 64), BF16) for i in range(2)]
                    mL = sb("mL", [64, 64], F32, sr)
                    mU = sb("mU", [64, 64], F32, sr)
                    mUi = sb("mUi", [64, 64], F32, sr)
                    P.dma("sp", mL[:], c_mL[:, 0, :], writes=[prm_t])
                    P.dma("act", mU[:], c_mU[:, 0, :], writes=[prm_t])
                    P.dma("pool", mUi[:], c_mUi[:, 0, :], writes=[prm_t])

                    def bc16(m):
                        return m[:, :].unsqueeze(1).to_broadcast([64, 16, 64])
                    psbig = ps("rbig", [64, 8, SW], F32, sr)
                    psbig_t = T("rbig")
                    psw = [ps("rw%d" % i, [64, 16, 64], F32, sr) for i in range(2)]
                    psw_t = [T("rw%d" % i) for i in range(2)]
                    psq = [ps("rq%d" % i, [64, 8, 64], F32, sr) for i in range(2)]
                    psq_t = [T("rq%d" % i) for i in range(2)]
                    wn = [0]

                    def nextw():
                        i = wn[0] % 2
                        wn[0] += 1
                        return psw[i], psw_t[i]
                    P.barrier()
                    qn = [0]

                    def nextq():
                        i = qn[0] % 2
                        qn[0] += 1
                        return psq[i], psq_t[i]

                    def bc8(col8, w=SW):
                        return col8.unsqueeze(2).to_broadcast([64, 8, w])

                    def flat(a):
                        return a[:].rearrange("p h t -> p (h t)")

                    def prepA(sc, Z):
                        R_, R_t = Z["R"]; K_, K_t = Z["K"]; V_, V_t = Z["V"]; KK, KK_t = Z["KK"]; A_, A_t = Z["A"]
                        G_, G_t = Z["G"]; SG, SG_t = Z["SG"]; CS, CS_t = Z["CS"]; D1, D1_t = Z["D1"]; EW, EW_t = Z["EW"]
                        EWI, EWI_t = Z["EWI"]; BON, BON_t = Z["BON"]
                        ABb, ABb_t = Z["ABb"]; BBb, BBb_t = Z["BBb"]; KBb, KBb_t = Z["KBb"]; RBb, RBb_t = Z["RBb"]; Vb, Vb_t = Z["Vb"]
                        t0 = sc * SW
                        srcs = [(rawR, rawR_t, RW0, 512, True), (rawK, rawK_t, RW0 + 512, 512, True),
                                (rawV, rawV_t, RW0 + 1024, 512, True), (rawW, rawW_t, RW0 + 1536, 64, False),
                                (rawA, rawA_t, RW0 + 1600, 64, False), (rawG, rawG_t, RW0 + 1664, 128, False)]
                        for qi, (buf, bt, r0, nr, hd) in enumerate(srcs):
                            lo = t0 - 1 if sc > 0 else 0
                            dlo = 0 if sc > 0 else 1
                            src = pT_d[r0:r0 + nr, lo:t0 + SW]
                            if hd:
                                if sc == 0:
                                    P.op("pool", lambda e, buf=buf: e.memset(buf[:, :, 0:1], 0.0), writes=[bt])
                                P.dma("sp", buf[:, :, dlo:SW + 1], src.rearrange("(h d) t -> d h t", d=64), writes=[bt])
                            else:
                                if sc == 0:
                                    P.op("pool", lambda e, buf=buf: e.memset(buf[:, 0:1], 0.0), writes=[bt])
                                P.dma("sp", buf[:, dlo:SW + 1], src, writes=[bt])
                        for ai, (raw, raw_t, dst, dst_t) in enumerate(((rawR, rawR_t, R_, R_t), (rawK, rawK_t, K_, K_t),
                                                                       (rawV, rawV_t, V_, V_t))):
                            eng = "dve" if ai != 1 else "pool"
                            mub = bc8(mu_rkv[:, ai * 8:(ai + 1) * 8])
                            P.op(eng, lambda e, raw=raw, dst=dst: e.tensor_tensor(
                                out=dst[:], in0=raw[:, :, 0:SW], in1=raw[:, :, 1:SW + 1], op=ALU.subtract),
                                reads=[raw_t], writes=[dst_t])
                            P.op(eng, lambda e, dst=dst, mub=mub: e.tensor_tensor(out=dst[:], in0=dst[:], in1=mub, op=ALU.mult),
                                 reads=[dst_t, prm_t], writes=[dst_t])
                            P.op(eng, lambda e, raw=raw, dst=dst: e.tensor_tensor(
                                out=dst[:], in0=dst[:], in1=raw[:, :, 1:SW + 1], op=ALU.add),
                                reads=[dst_t, raw_t], writes=[dst_t])
                        for (raw, raw_t, dst, dst_t, mu) in ((rawW, rawW_t, TW, TW_t, mu_w), (rawA, rawA_t, AL, AL_t, mu_a),
                                                              (rawG, rawG_t, GL, GL_t, mug[:, 0:1])):
                            P.op("dve", lambda e, raw=raw, dst=dst: e.tensor_tensor(
                                out=dst[:], in0=raw[:, 0:SW], in1=raw[:, 1:SW + 1], op=ALU.subtract),
                                reads=[raw_t], writes=[dst_t])
                            P.op("dve", lambda e, raw=raw, dst=dst, mu=mu: e.scalar_tensor_tensor(
                                out=dst[:], in0=dst[:], scalar=mu, in1=raw[:, 1:SW + 1], op0=ALU.mult, op1=ALU.add),
                                reads=[dst_t, raw_t, prm_t], writes=[dst_t])
                        P.op("act", lambda e: e.activation(out=TWb[:], in_=TW[:], func=AF.Tanh), reads=[TW_t], writes=[TWb_t])
                        P.op("act", lambda e: e.activation(out=GLb[:], in_=GL[:], func=AF.Sigmoid), reads=[GL_t], writes=[GLb_t])
                        P.op("pool", lambda e: e.tensor_copy(out=ALb[:], in_=AL[:]), reads=[AL_t], writes=[ALb_t])
                        for h in range(8):
                            P.op("pe", lambda e, h=h: e.matmul(psbig[:, h, :], lhsT=w2b[:, h * 64:(h + 1) * 64], rhs=TWb[:],
                                                               start=True, stop=True),
                                 reads=[TWb_t, prm_t], writes=[psbig_t])
                        P.op("dve", lambda e: e.tensor_tensor(out=SG[:], in0=psbig[:], in1=bc8(w0c), op=ALU.add),
                             reads=[psbig_t, prm_t], writes=[SG_t])
                        P.op("act", lambda e: e.activation(out=SG[:], in_=SG[:], func=AF.Sigmoid), reads=[SG_t], writes=[SG_t])
                        for h in range(8):
                            P.op("pe", lambda e, h=h: e.matmul(psbig[:, h, :], lhsT=a2b[:, h * 64:(h + 1) * 64], rhs=ALb[:],
                                                               start=True, stop=True),
                                 reads=[ALb_t, prm_t], writes=[psbig_t])
                        P.op("dve", lambda e: e.tensor_tensor(out=A_[:], in0=psbig[:], in1=bc8(a0c), op=ALU.add),
                             reads=[psbig_t, prm_t], writes=[A_t])
                        P.op("act", lambda e: e.activation(out=A_[:], in_=A_[:], func=AF.Sigmoid), reads=[A_t], writes=[A_t])
                        for h in range(8):
                            P.op("pe", lambda e, h=h: e.matmul(psbig[:, h, :], lhsT=g2b[:, h * 64:(h + 1) * 64], rhs=GLb[:],
                                                               start=True, stop=True),
                                 reads=[GLb_t, prm_t], writes=[psbig_t])
                        P.op("act", lambda e: e.activation(out=G_[:], in_=psbig[:], func=AF.Identity),
                             reads=[psbig_t], writes=[G_t])
                        P.op("pool", lambda e: e.tensor_tensor(out=KK[:], in0=K_[:], in1=bc8(kkc), op=ALU.mult),
                             reads=[K_t, prm_t], writes=[KK_t])
                        P.op("pool", lambda e: e.tensor_tensor(out=D1b[:], in0=KK[:], in1=KK[:], op=ALU.mult),
                             reads=[KK_t], writes=[D1b_t])
                        for hp in range(2):
                            P.op("pe", lambda e, hp=hp: e.matmul(
                                psbig[:, hp * 4:(hp + 1) * 4, :].rearrange("p h t -> p (h t)"), lhsT=ones_bf[0:64, 0:64],
                                rhs=D1b[:, hp * 4:(hp + 1) * 4, :].rearrange("p h t -> p (h t)"), start=True, stop=True),
                                reads=[D1b_t, ones_t], writes=[psbig_t])
                        P.op("act", lambda e: e.activation(out=D1[:], in_=psbig[:], func=AF.Ln, bias=eps_col[0:64, 4:5], scale=1.0),
                             reads=[psbig_t, eps_t], writes=[D1_t])
                        P.op("act", lambda e: e.activation(out=D1[:], in_=D1[:], func=AF.Exp, scale=-0.5), reads=[D1_t], writes=[D1_t])
                        P.op("dve", lambda e: e.tensor_tensor(out=KK[:], in0=KK[:], in1=D1[:], op=ALU.mult),
                             reads=[KK_t, D1_t], writes=[KK_t])
                        P.op("pool", lambda e: e.tensor_tensor(out=D1[:], in0=A_[:], in1=bc8(kac), op=ALU.mult),
                             reads=[A_t, prm_t], writes=[D1_t])
                        P.op("pool", lambda e: e.tensor_tensor(out=D1[:], in0=D1[:], in1=bc8(omka[:, :]), op=ALU.add),
                             reads=[D1_t, prm_t], writes=[D1_t])
                        P.op("pool", lambda e: e.tensor_tensor(out=K_[:], in0=K_[:], in1=D1[:], op=ALU.mult),
                             reads=[K_t, D1_t], writes=[K_t])
                        P.op("dve", lambda e: e.tensor_tensor(out=D1[:], in0=R_[:], in1=K_[:], op=ALU.mult),
                             reads=[R_t, K_t], writes=[D1_t])
                        P.op("dve", lambda e: e.tensor_tensor(out=D1b[:], in0=D1[:], in1=bc8(rkc), op=ALU.mult),
                             reads=[D1_t, prm_t], writes=[D1b_t])
                        for hp in range(2):
                            P.op("pe", lambda e, hp=hp: e.matmul(
                                psbig[:, hp * 4:(hp + 1) * 4, :].rearrange("p h t -> p (h t)"), lhsT=ones_bf[0:64, 0:64],
                                rhs=D1b[:, hp * 4:(hp + 1) * 4, :].rearrange("p h t -> p (h t)"), start=True, stop=True),
                                reads=[D1b_t, ones_t], writes=[psbig_t])
                        P.op("dve", lambda e: e.tensor_tensor(out=BON[:], in0=psbig[:], in1=V_[:], op=ALU.mult),
                             reads=[psbig_t, V_t], writes=[BON_t])
                        P.op("dve", lambda e: e.tensor_tensor_scan(out=flat(CS), data0=flat(rst), data1=flat(SG), initial=0.0,
                                                                   op0=ALU.mult, op1=ALU.add),
                             reads=[SG_t, prm_t], writes=[CS_t])
                        P.op("act", lambda e: e.activation(out=EW[:], in_=CS[:], func=AF.Exp, scale=-CDEC), reads=[CS_t], writes=[EW_t])
                        P.op("act", lambda e: e.activation(out=EWI[:], in_=CS[:], func=AF.Exp, scale=CDEC), reads=[CS_t], writes=[EWI_t])
                        P.op("dve", lambda e: e.tensor_tensor(out=D1[:], in0=CS[:], in1=SG[:], op=ALU.subtract),
                             reads=[CS_t, SG_t], writes=[D1_t])
                        P.op("act", lambda e: e.activation(out=D1[:], in_=D1[:], func=AF.Exp, scale=-CDEC), reads=[D1_t], writes=[D1_t])
                        P.op("dve", lambda e: e.tensor_tensor(out=R_[:], in0=R_[:], in1=EW[:], op=ALU.mult),
                             reads=[R_t, EW_t], writes=[R_t])
                        P.op("pool", lambda e: e.tensor_tensor(out=K_[:], in0=K_[:], in1=EWI[:], op=ALU.mult),
                             reads=[K_t, EWI_t], writes=[K_t])
                        P.op("dve", lambda e: e.tensor_tensor(out=A_[:], in0=KK[:], in1=A_[:], op=ALU.mult),
                             reads=[KK_t, A_t], writes=[A_t])
                        P.op("dve", lambda e: e.tensor_tensor(out=BBb[:], in0=A_[:], in1=EWI[:], op=ALU.mult),
                             reads=[A_t, EWI_t], writes=[BBb_t])
                        P.op("dve", lambda e: e.scalar_tensor_tensor(out=ABb[:], in0=KK[:], scalar=-1.0, in1=D1[:],
                                                                     op0=ALU.mult, op1=ALU.mult),
                             reads=[KK_t, D1_t], writes=[ABb_t])

                        P.op("act", lambda e: e.activation(out=KBb[:], in_=K_[:], func=AF.Identity), reads=[K_t], writes=[KBb_t])
                        P.op("act", lambda e: e.activation(out=RBb[:], in_=R_[:], func=AF.Identity), reads=[R_t], writes=[RBb_t])
                        P.op("pool", lambda e: e.tensor_copy(out=Vb[:], in_=V_[:]), reads=[V_t], writes=[Vb_t])

                    def runB(sc, Z):
                        R_, R_t = Z["R"]; K_, K_t = Z["K"]; V_, V_t = Z["V"]; KK, KK_t = Z["KK"]; A_, A_t = Z["A"]
                        G_, G_t = Z["G"]; SG, SG_t = Z["SG"]; CS, CS_t = Z["CS"]; D1, D1_t = Z["D1"]; EW, EW_t = Z["EW"]
                        EWI, EWI_t = Z["EWI"]; BON, BON_t = Z["BON"]
                        ABb, ABb_t = Z["ABb"]; BBb, BBb_t = Z["BBb"]; KBb, KBb_t = Z["KBb"]; RBb, RBb_t = Z["RBb"]; Vb, Vb_t = Z["Vb"]
                        t0 = sc * SW
                        def bview(a, half):
                            return a[:].rearrange("p h t -> p (h t)").bitcast(BF16)[:, half * 1024:(half + 1) * 1024].rearrange(
                                "p (g i) -> p g i", i=64)
                        Vt, Vt_t = bview(SG, 0), SG_t
                        KBt, KBt_t = bview(CS, 0), CS_t
                        BBt, BBt_t = bview(EWI, 0), EWI_t
                        Lak, Lak_t = bview(A_, 0), A_t
                        Qrb, Qrb_t = bview(KK, 0), KK_t
                        Qrk, Qrk_t = bview(D1, 0), D1_t
                        NCH = SW // CH

                        def csl_(cc):
                            return slice(cc * CH, (cc + 1) * CH)

                        for (src, src_t, dst, dst_t) in ((Vb, Vb_t, Vt, Vt_t), (KBb, KBb_t, KBt, KBt_t), (BBb, BBb_t, BBt, BBt_t)):
                            pw, pw_t = nextw()
                            pwb = pw[:].rearrange("p g i -> p (g i)").bitcast(BF16)[:, 0:1024].rearrange("p (g i) -> p g i", i=64)
                            for cc in range(NCH):
                                for h in range(8):
                                    P.op("pe", lambda e, pwb=pwb, src=src, h=h, cc=cc: e.transpose(
                                        pwb[:, cc * 8 + h, :], src[:, h, csl_(cc)], idb[0:64, 0:64]),
                                        reads=[src_t, ones_t], writes=[pw_t])
                            P.op("act", lambda e, pwb=pwb, dst=dst: e.activation(out=dst, in_=pwb, func=AF.Identity),
                                 reads=[pw_t], writes=[dst_t])
                        X0, X0_t = Xs[0]
                        Y0, Y0_t = Ys[0]
                        prods = [(ABb, ABb_t, BBb, BBb_t, Y0[:], Y0_t, mL),
                                 (BBb, BBb_t, ABb, ABb_t, X0[:], X0_t, mU),
                                 (KBb, KBb_t, ABb, ABb_t, Lak, Lak_t, mU),
                                 (BBb, BBb_t, RBb, RBb_t, Qrb, Qrb_t, mUi),
                                 (KBb, KBb_t, RBb, RBb_t, Qrk, Qrk_t, mUi)]
                        for pi_, (la, la_t, ra, ra_t, dst, dst_t, msk) in enumerate(prods):
                            pw, pw_t = nextw()
                            for cc in range(NCH):
                                for h in range(8):
                                    P.op("pe", lambda e, pw=pw, la=la, ra=ra, h=h, cc=cc: e.matmul(
                                        pw[:, cc * 8 + h, :], lhsT=la[:, h, csl_(cc)], rhs=ra[:, h, csl_(cc)], start=True, stop=True),
                                        reads=[la_t, ra_t], writes=[pw_t])
                            P.op("dve", lambda e, pw=pw, dst=dst, msk=msk: e.tensor_tensor(
                                out=dst, in0=pw[:], in1=bc16(msk), op=ALU.mult),
                                reads=[pw_t, prm_t], writes=[dst_t])
                        A0, A0_t = As[0]
                        B0, B0_t = Bs[0]
                        P.op("dve", lambda e: e.tensor_tensor(out=A0[:], in0=X0[:], in1=idb[0:64, 0:64].unsqueeze(1).to_broadcast([64, 16, 64]),
                                                              op=ALU.add), reads=[X0_t, ones_t], writes=[A0_t])
                        P.op("pool", lambda e: e.tensor_tensor(out=B0[:], in0=Y0[:], in1=idb[0:64, 0:64].unsqueeze(1).to_broadcast([64, 16, 64]),
                                                               op=ALU.add), reads=[Y0_t, ones_t], writes=[B0_t])
                        for j in range(5):
                            Xc, Xc_t = Xs[j % 2]
                            Yc, Yc_t = Ys[j % 2]
                            Xn, Xn_t = Xs[(j + 1) % 2]
                            Yn, Yn_t = Ys[(j + 1) % 2]
                            Ac, Ac_t = As[j % 2]
                            Bc, Bc_t = Bs[j % 2]
                            An, An_t = As[(j + 1) % 2]
                            Bn, Bn_t = Bs[(j + 1) % 2]
                            pw, pw_t = nextw()
                            for g in range(16):
                                P.op("pe", lambda e, pw=pw, Yc=Yc, Xc=Xc, g=g: e.matmul(
                                    pw[:, g, :], lhsT=Yc[:, g, :], rhs=Xc[:, g, :], start=True, stop=True),
                                    reads=[Yc_t, Xc_t], writes=[pw_t])
                            P.op("act", lambda e, pw=pw, Xn=Xn: e.activation(out=Xn[:], in_=pw[:], func=AF.Identity),
                                 reads=[pw_t], writes=[Xn_t])
                            if j < 4:
                                pw, pw_t = nextw()
                                for g in range(16):
                                    P.op("pe", lambda e, pw=pw, Yc=Yc, Xc=Xc, g=g: e.matmul(
                                        pw[:, g, :], lhsT=Xc[:, g, :], rhs=Yc[:, g, :], start=True, stop=True),
                                        reads=[Yc_t, Xc_t], writes=[pw_t])
                                P.op("act", lambda e, pw=pw, Yn=Yn: e.activation(out=Yn[:], in_=pw[:], func=AF.Identity),
                                     reads=[pw_t], writes=[Yn_t])
                            pw, pw_t = nextw()
                            for g in range(16):
                                P.op("pe", lambda e, pw=pw, Bc=Bc, Xn=Xn, g=g: e.matmul(
                                    pw[:, g, :], lhsT=Bc[:, g, :], rhs=Xn[:, g, :], start=True, stop=False),
                                    reads=[Bc_t, Xn_t], writes=[pw_t])
                                P.op("pe", lambda e, pw=pw, Ac=Ac, g=g: e.matmul(
                                    pw[:, g, :], lhsT=idb[0:64, 0:64], rhs=Ac[:, g, :], start=False, stop=True),
                                    reads=[Ac_t, ones_t], writes=[pw_t])
                            if j < 4:
                                P.op("act", lambda e, pw=pw, An=An: e.activation(out=An[:], in_=pw[:], func=AF.Identity),
                                     reads=[pw_t], writes=[An_t])
                                pw, pw_t = nextw()
                                for g in range(16):
                                    P.op("pe", lambda e, pw=pw, Bc=Bc, Xn=Xn, g=g: e.matmul(
                                        pw[:, g, :], lhsT=Xn[:, g, :], rhs=Bc[:, g, :], start=True, stop=False),
                                        reads=[Bc_t, Xn_t], writes=[pw_t])
                                    P.op("pe", lambda e, pw=pw, Bc=Bc, g=g: e.matmul(
                                        pw[:, g, :], lhsT=idb[0:64, 0:64], rhs=Bc[:, g, :], start=False, stop=True),
                                        reads=[Bc_t, ones_t], writes=[pw_t])
                                P.op("act", lambda e, pw=pw, Bn=Bn: e.activation(out=Bn[:], in_=pw[:], func=AF.Identity),
                                     reads=[pw_t], writes=[Bn_t])
                            else:
                                P.op("act", lambda e, pw=pw: e.activation(out=TT[:], in_=pw[:], func=AF.Identity),
                                     reads=[pw_t], writes=[TT_t])

                        def do_chunk(cc, csl):
                            g0 = cc * 8
                            P.op("pool", lambda e: e.tensor_copy(out=STb[:], in_=ST[:]), reads=[ST_t], writes=[STb_t])
                            pq, pq_t = nextq()
                            for h in range(8):
                                P.op("pe", lambda e, pq=pq, h=h: e.matmul(pq[:, h, :], lhsT=ABb[:, h, csl], rhs=STb[:, h, :],
                                                                          start=True, stop=False),
                                     reads=[ABb_t, STb_t], writes=[pq_t])
                                P.op("pe", lambda e, pq=pq, h=h: e.matmul(pq[:, h, :], lhsT=Lak[:, g0 + h, :], rhs=Vt[:, g0 + h, :],
                                                                          start=False, stop=True),
                                     reads=[Lak_t, Vt_t], writes=[pq_t])
                            P.op("act", lambda e, pq=pq: e.activation(out=Z_[:], in_=pq[:], func=AF.Identity),
                                 reads=[pq_t], writes=[Z_t])
                            pq, pq_t = nextq()
                            for h in range(8):
                                P.op("pe", lambda e, pq=pq, h=h: e.matmul(pq[:, h, :], lhsT=TT[:, g0 + h, :], rhs=Z_[:, h, :],
                                                                          start=True, stop=True),
                                     reads=[TT_t, Z_t], writes=[pq_t])
                            P.op("act", lambda e, pq=pq: e.activation(out=U_[:], in_=pq[:], func=AF.Identity),
                                 reads=[pq_t], writes=[U_t])
                            pq, pq_t = nextq()
                            for h in range(8):
                                P.op("pe", lambda e, pq=pq, h=h: e.matmul(pq[:, h, :], lhsT=STb[:, h, :], rhs=RBb[:, h, csl],
                                                                          start=True, stop=False),
                                     reads=[STb_t, RBb_t], writes=[pq_t])
                                P.op("pe", lambda e, pq=pq, h=h: e.matmul(pq[:, h, :], lhsT=U_[:, h, :], rhs=Qrb[:, g0 + h, :],
                                                                          start=False, stop=False),
                                     reads=[U_t, Qrb_t], writes=[pq_t])
                                P.op("pe", lambda e, pq=pq, h=h: e.matmul(pq[:, h, :], lhsT=Vt[:, g0 + h, :], rhs=Qrk[:, g0 + h, :],
                                                                          start=False, stop=True),
                                     reads=[Vt_t, Qrk_t], writes=[pq_t])
                            P.op("act", lambda e, pq=pq: e.activation(out=YR[:, :, csl], in_=pq[:], func=AF.Identity),
                                 reads=[pq_t], writes=[YR_t])
                            pq, pq_t = nextq()
                            for h in range(8):
                                P.op("pe", lambda e, pq=pq, h=h: e.matmul(pq[:, h, :], lhsT=BBt[:, g0 + h, :], rhs=U_[:, h, :],
                                                                          start=True, stop=False),
                                     reads=[BBt_t, U_t], writes=[pq_t])
                                P.op("pe", lambda e, pq=pq, h=h: e.matmul(pq[:, h, :], lhsT=KBt[:, g0 + h, :], rhs=Vt[:, g0 + h, :],
                                                                          start=False, stop=True),
                                     reads=[KBt_t, Vt_t], writes=[pq_t])
                            P.op("dve", lambda e, pq=pq: e.tensor_tensor(out=ST[:], in0=ST[:], in1=pq[:], op=ALU.add),
                                 reads=[pq_t, ST_t], writes=[ST_t])
                            last = cc * CH + CH - 1
                            P.op("dve", lambda e, last=last: e.tensor_tensor(
                                out=ST[:], in0=ST[:], in1=EW[:, :, last:last + 1].to_broadcast([64, 8, 64]), op=ALU.mult),
                                reads=[ST_t, EW_t], writes=[ST_t])

                        for cc in range(SW // CH):
                            do_chunk(cc, slice(cc * CH, (cc + 1) * CH))

                        def headsum(src, src_t):
                            pw, pw_t = nextw()
                            pwv = pw[:].rearrange("p g i -> p (g i)").rearrange("p (h t) -> p h t", t=SW)
                            for hp in range(2):
                                P.op("pe", lambda e, hp=hp, pwv=pwv: e.matmul(
                                    pwv[:, hp * 4:(hp + 1) * 4, :].rearrange("p h t -> p (h t)"), lhsT=mean_b[:, :],
                                    rhs=src[:, hp * 4:(hp + 1) * 4, :].rearrange("p h t -> p (h t)"), start=True, stop=True),
                                    reads=[src_t, ones_t], writes=[pw_t])
                            return pwv, pw_t
                        P.op("act", lambda e: e.activation(out=YRb[:], in_=YR[:], func=AF.Identity), reads=[YR_t], writes=[YRb_t])
                        pwv, pw_t = headsum(YRb, YRb_t)
                        P.op("dve", lambda e, pwv=pwv: e.tensor_tensor(out=YR[:], in0=YR[:], in1=pwv, op=ALU.subtract),
                             reads=[YR_t, pw_t], writes=[YR_t])
                        P.op("pool", lambda e: e.tensor_tensor(out=D1b[:], in0=YR[:], in1=YR[:], op=ALU.mult),
                             reads=[YR_t], writes=[D1b_t])
                        pwv2, pw2_t = headsum(D1b, D1b_t)
                        P.op("act", lambda e, pwv2=pwv2: e.activation(out=D1[:], in_=pwv2, func=AF.Ln, bias=eps_col[0:64, 1:2], scale=1.0),
                             reads=[pw2_t, eps_t], writes=[D1_t])
                        P.op("act", lambda e: e.activation(out=D1[:], in_=D1[:], func=AF.Exp, scale=-0.5), reads=[D1_t], writes=[D1_t])
                        P.op("dve", lambda e: e.tensor_tensor(out=YR[:], in0=YR[:], in1=D1[:], op=ALU.mult),
                             reads=[YR_t, D1_t], writes=[YR_t])
                        P.op("dve", lambda e: e.tensor_tensor(out=YR[:], in0=YR[:], in1=bc8(lnw), op=ALU.mult),
                             reads=[YR_t, prm_t], writes=[YR_t])
                        P.op("dve", lambda e: e.tensor_tensor(out=YR[:], in0=YR[:], in1=bc8(lnb), op=ALU.add),
                             reads=[YR_t, prm_t], writes=[YR_t])
                        P.op("pool", lambda e: e.tensor_tensor(out=YR[:], in0=YR[:], in1=BON[:], op=ALU.add),
                             reads=[YR_t, BON_t], writes=[YR_t])
                        P.op("pool", lambda e: e.tensor_tensor(out=YO[:], in0=YR[:], in1=G_[:], op=ALU.mult),
                             reads=[YR_t, G_t], writes=[YO_t])
                        P.dma("sp", yT_d[256:768, t0:t0 + SW].rearrange("(h d) t -> d h t", d=64), YO[:], reads=[YO_t])

                    for j in range(7):
                        P.dma(dq[j % 3], xs_d[j], xT[0:64, j, :], reads=xT_t[j])
                    P.barrier()
                    NSC = S // SW
                    P.begin_capture()
                    prepA(0, sets[0])
                    P.play(P.end_capture())
                    for sc in range(NSC):
                        la = []
                        if sc + 1 < NSC:
                            P.begin_capture()
                            prepA(sc + 1, sets[(sc + 1) % 2])
                            la = P.end_capture()
                        P.begin_capture()
                        runB(sc, sets[sc % 2])
                        lb = P.end_capture()
                        ksp = int(len(lb) * 0.45)
                        P.play(lb[:ksp])
                        P.play(lb[ksp:], la)
                    P.barrier()
                    for j in range(7):
                        P.dma(dq[j % 3], xT[0:64, j, :], xs_d[j], writes=xT_t[j])
                    P.barrier()

            if "rwkv" not in skip:
                rwkv()

            if dbg == "rw":
                return True
            if dbg == "yT" and l == nlayers - 1:
                with contextlib.ExitStack() as sd:
                    yb = sb("dbg_y", [128, 8, S], BF16, sd)
                    yb_t = T("dbg_y")
                    P.dma("sp", yb[:], yT_d.rearrange("(j p) t -> p j t", p=128), writes=[yb_t])
                    P.dma("sp", dbg_d.rearrange("(j p) t -> p j t", p=128), yb[:], reads=[yb_t])
                    P.barrier()
                return True

            with contextlib.ExitStack() as so:
                yTs = sb("yTs", [128, 8, S], BF16, so)
                yTs_t = [T("yTs%d" % j) for j in range(8)]
                for j in range(8):
                    P.dma(dq[j % 3], yTs[:, j, :], yT_d[j * 128:(j + 1) * 128, :], writes=[yTs_t[j]])
                owst = [sb("oowst%d" % i, [128, 8, 128], F32, so) for i in range(2)]
                owst_t = [T("oowst%d" % i) for i in range(2)]
                owbf = [sb("oowbf%d" % i, [128, 8, 128], BF16, so) for i in range(2)]
                owbf_t = [T("oowbf%d" % i) for i in range(2)]
                opp = [ps("opp%d" % i, [128, S], F32, so) for i in range(2)]
                opp_t = [[T("opp%d_%d" % (i, c)) for c in range(4)] for i in range(2)]
                wov = wout_d[l].rearrange("(k p) n -> p k n", p=128)
                for m in range(8):
                    b = m % 2
                    P.dma("sp", owst[b][:], wov[:, :, m * 128:(m + 1) * 128], writes=[owst_t[b]])
                    P.op("pool", lambda e, b=b: e.tensor_copy(out=owbf[b][:], in_=owst[b][:]), reads=[owst_t[b]], writes=[owbf_t[b]])
                    for c in range(4):
                        cs = slice(c * 512, (c + 1) * 512)
                        for k in range(8):
                            P.op("pe", lambda e, b=b, k=k, cs=cs: e.matmul(
                                opp[b][:, cs], lhsT=owbf[b][:, k, :], rhs=yTs[:, k, cs], start=(k == 0), stop=(k == 7)),
                                reads=[owbf_t[b], yTs_t[k]], writes=[opp_t[b][c]])
                        P.op("dve", lambda e, b=b, m=m, cs=cs: e.scalar_tensor_tensor(
                            out=xT[:, m, cs], in0=opp[b][:, cs], scalar=modT[:, 16 + m:17 + m], in1=xT[:, m, cs],
                            op0=ALU.mult, op1=ALU.add),
                            reads=[opp_t[b][c], mod_t, xT_t[m][c]], writes=[xT_t[m][c]])
                P.barrier()

            if dbg == "xm" and l == nlayers - 1:
                for j in range(8):
                    P.dma(dq[j % 3], dbg_d[j * 128:(j + 1) * 128, :], xT[:, j, :], reads=xT_t[j])
                P.barrier()
                return True

            with contextlib.ExitStack() as sf:
                h2T = sb("h2T", [128, 8, S], BF16, sf)
                h2T_t = [[T("h2T%d_%d" % (j, c)) for c in range(4)] for j in range(8)]

                def sink_h2(j, c, cs, tp, tp_t, g, sh, par_t):
                    P.op("act", lambda e: e.activation(out=h2T[:, j, cs], in_=tp[:], func=AF.Identity, bias=sh, scale=g),
                         reads=[tp_t] + par_t, writes=[h2T_t[j][c]])

                norm_stage(lambda j: gm[:, 8 + j:9 + j], lambda j: modT[:, 24 + j:25 + j], [gm_t, mod_t], sink_h2)
                with contextlib.ExitStack() as su:
                    cwc = sb("cwc", [128, 3, 44], F32, su)
                    cbc = sb("cbc", [128, 44], F32, su)
                    cw_t = T("cwc")
                    P.dma("pool", cwc[:], cw_d[l], writes=[cw_t])
                    P.dma("pool", cbc[:], cb_d[l], writes=[cw_t])
                    uwst = [sb("uuwst%d" % i, [128, 8, 256], F32, su) for i in range(2)]
                    uwst_t = [T("uuwst%d" % i) for i in range(2)]
                    uwbf = [sb("uuwbf%d" % i, [128, 8, 256], BF16, su) for i in range(2)]
                    uwbf_t = [T("uuwbf%d" % i) for i in range(2)]
                    pu = [[ps("pu%d_%d" % (gv, hf), [128, 1024], F32, su) for hf in range(2)] for gv in range(2)]
                    pu_t = [[[T("pu%d_%d_%d" % (gv, hf, c)) for c in range(2)] for hf in range(2)] for gv in range(2)]
                    ugh = [[sb("ug%d_%d" % (gv, hf), [128, 1026], F32, su) for hf in range(2)] for gv in range(2)]
                    ugh_t = [[T("ug%d_%d" % (gv, hf)) for hf in range(2)] for gv in range(2)]
                    cgh = [[sb("cg%d_%d" % (gv, hf), [128, 1024], F32, su) for hf in range(2)] for gv in range(2)]
                    cgh_t = [[T("cg%d_%d" % (gv, hf)) for hf in range(2)] for gv in range(2)]
                    zo = [sb("zo%d" % i, [128, S], BF16, su) for i in range(2)]
                    zo_t = [[T("zo%d_%d" % (i, hf)) for hf in range(2)] for i in range(2)]
                    for gv in range(2):
                        P.op("pool", lambda e, gv=gv: e.memset(ugh[gv][0][:, 0:2], 0.0), writes=[ugh_t[gv][0]])
                    wuv = wup_d[l].rearrange("(k p) n -> p k n", p=128)

                    def ffn_prefetch(jf):
                        b = jf % 2
                        P.dma("sp", uwst[b][:, :, 0:128], wuv[:, :, jf * 128:(jf + 1) * 128], writes=[uwst_t[b]])
                        P.dma("sp", uwst[b][:, :, 128:256], wuv[:, :, DFF + jf * 128:DFF + (jf + 1) * 128], writes=[uwst_t[b]])
                        P.op("pool", lambda e: e.tensor_copy(out=uwbf[b][:], in_=uwst[b][:]), reads=[uwst_t[b]], writes=[uwbf_t[b]])

                    def ffn_tile(jf):
                        b = jf % 2
                        if jf + 1 < 22:
                            ffn_prefetch(jf + 1)
                        for hf in range(2):
                            for gv in range(2):
                                for c2 in range(2):
                                    c = hf * 2 + c2
                                    cs = slice(c * 512, (c + 1) * 512)
                                    for k in range(8):
                                        P.op("pe", lambda e, gv=gv, hf=hf, c2=c2, k=k, cs=cs: e.matmul(
                                            pu[gv][hf][:, c2 * 512:(c2 + 1) * 512], lhsT=uwbf[b][:, k, gv * 128:(gv + 1) * 128],
                                            rhs=h2T[:, k, cs], start=(k == 0), stop=(k == 7)),
                                            reads=[uwbf_t[b], h2T_t[k][c]], writes=[pu_t[gv][hf][c2]])
                                P.op("act", lambda e, gv=gv, hf=hf: e.activation(
                                    out=ugh[gv][hf][:, 2:1026], in_=pu[gv][hf][:, :], func=AF.Identity),
                                    reads=pu_t[gv][hf], writes=[ugh_t[gv][hf]])
                                if hf == 0:
                                    P.op("act", lambda e, gv=gv: e.activation(
                                        out=ugh[gv][1][:, 0:2], in_=pu[gv][0][:, 1022:1024], func=AF.Identity),
                                        reads=pu_t[gv][0], writes=[ugh_t[gv][1]])
                            for gv in range(2):
                                col = gv * 22 + jf
                                P.op("act", lambda e, gv=gv, hf=hf, col=col: e.activation(
                                    out=cgh[gv][hf][:], in_=pu[gv][hf][:, :], func=AF.Identity, bias=cbc[:, col:col + 1],
                                    scale=cwc[:, 2, col:col + 1]),
                                    reads=pu_t[gv][hf] + [cw_t], writes=[cgh_t[gv][hf]])
                                for tap in (0, 1):
                                    P.op("dve", lambda e, gv=gv, hf=hf, col=col, tap=tap: e.scalar_tensor_tensor(
                                        out=cgh[gv][hf][:], in0=ugh[gv][hf][:, tap:tap + 1024], scalar=cwc[:, tap, col:col + 1],
                                        in1=cgh[gv][hf][:], op0=ALU.mult, op1=ALU.add),
                                        reads=[ugh_t[gv][hf], cw_t, cgh_t[gv][hf]], writes=[cgh_t[gv][hf]])
                            P.op("act", lambda e, hf=hf: e.activation(out=cgh[0][hf][:], in_=cgh[0][hf][:], func=AF.Silu),
                                 reads=[cgh_t[0][hf]], writes=[cgh_t[0][hf]])
                            P.op("dve", lambda e, hf=hf: e.tensor_tensor(
                                out=zo[b][:, hf * 1024:(hf + 1) * 1024], in0=cgh[0][hf][:], in1=cgh[1][hf][:], op=ALU.mult),
                                reads=[cgh_t[0][hf], cgh_t[1][hf]], writes=[zo_t[b][hf]])
                        P.dma("act", zT_d[jf * 128:(jf + 1) * 128, :], zo[b][:], reads=zo_t[b])

                    ffn_prefetch(0)
                    for jf in range(22):
                        ffn_tile(jf)
                    P.barrier()

            with contextlib.ExitStack() as sdn:
                wd = sb("wd", [128, 22, D], BF16, sdn)
                wd_t = [T("wd%d" % j) for j in range(22)]
                dst_ = [sb("dst%d" % i, [128, D], F32, sdn) for i in range(2)]
                dst_t = [T("dst%d" % i) for i in range(2)]
                for jf in range(22):
                    b = jf % 2
                    P.dma(dq[jf % 2], dst_[b][:], wdn_d[l][jf * 128:(jf + 1) * 128, :], writes=[dst_t[b]])
                    P.op("pool" if jf % 2 == 0 else "dve", lambda e, b=b, jf=jf: e.tensor_copy(out=wd[:, jf, :], in_=dst_[b][:]),
                         reads=[dst_t[b]], writes=[wd_t[jf]])
                zc = [sb("zc%d" % i, [128, 22, 512], BF16, sdn) for i in range(2)]
                zc_t = [T("zc%d" % i) for i in range(2)]
                pd = [ps("pd%d" % i, [128, 512], F32, sdn) for i in range(3)]
                pd_t = [T("pd%d" % i) for i in range(3)]
                n = 0
                for c in range(4):
                    cs = slice(c * 512, (c + 1) * 512)
                    b = c % 2
                    P.dma("sp", zc[b][:], zT_d[:, cs].rearrange("(j p) t -> p j t", p=128), writes=[zc_t[b]])
                    for m in range(8):
                        i = n % 3
                        n += 1
                        for jf in range(22):
                            P.op("pe", lambda e, i=i, b=b, m=m, jf=jf: e.matmul(
                                pd[i][:], lhsT=wd[:, jf, m * 128:(m + 1) * 128], rhs=zc[b][:, jf, :],
                                start=(jf == 0), stop=(jf == 21)),
                                reads=[wd_t[jf], zc_t[b]], writes=[pd_t[i]])
                        P.op("dve", lambda e, i=i, m=m, cs=cs: e.scalar_tensor_tensor(
                            out=xT[:, m, cs], in0=pd[i][:], scalar=modT[:, 40 + m:41 + m], in1=xT[:, m, cs],
                            op0=ALU.mult, op1=ALU.add),
                            reads=[pd_t[i], mod_t, xT_t[m][c]], writes=[xT_t[m][c]])
                P.barrier()

            if dbg == "xf" and l == nlayers - 1:
                for j in range(8):
                    P.dma(dq[j % 3], dbg_d[j * 128:(j + 1) * 128, :], xT[:, j, :], reads=xT_t[j])
                P.barrier()
                return True


        for l in range(nlayers):
            if layer_body(l):
                break

        if dbg is None:
            with contextlib.ExitStack() as sfn:
                fo = [sb("fo%d" % i, [128, 512], F32, sfn) for i in range(3)]
                fo_t = [T("fo%d" % i) for i in range(3)]
                cnt = [0]

                def sink_f(j, c, cs, tp, tp_t, g, sh, par_t):
                    i = cnt[0] % 3
                    cnt[0] += 1
                    P.op("act", lambda e: e.activation(out=fo[i][:], in_=tp[:], func=AF.Identity, scale=g),
                         reads=[tp_t] + par_t, writes=[fo_t[i]])
                    P.dma(dq[i % 2], out_d[j * 128:(j + 1) * 128, cs], fo[i][:], reads=[fo_t[i]])

                norm_stage(lambda j: nfin[:, j:j + 1], lambda j: None, [nfin_t], sink_f)

        P.barrier()
        P.emit()
    return nc


_CACHE = {}
_CONST = {}


def _consts():
    if _CONST:
        return _CONST
    bf = ml_dtypes.bfloat16
    s_ = np.arange(128)[:, None]
    q_ = np.arange(512)[None, :]
    cb = np.stack([np.where(j * 128 + s_ <= q_, 0.0, NEG) for j in range(4)], axis=1)
    _CONST["c_cb"] = cb.astype(bf)
    _CONST["c_idb"] = np.eye(128).astype(bf)
    _CONST["c_idf"] = np.eye(128, dtype=np.float32)
    sel = np.zeros((64, 4, 4), np.float32)
    for h in range(4):
        sel[:, h, h] = 1.0
    _CONST["c_sel"] = sel.astype(bf)
    rows = np.zeros((12, S), np.float32)
    rows[0:3] = 1.0
    rows[3:6] = -1.0
    rows[6] = 1.0
    _CONST["c_rows"] = rows.astype(bf)
    oneh = np.zeros((8, S), np.float32)
    for n in range(8):
        oneh[n, n * 256:(n + 1) * 256] = 1.0
    _CONST["c_oneh"] = oneh.astype(bf)
    own = (np.arange(16) // 2)[:, None]
    nn = np.arange(8)[None, :]
    negm = np.where(nn < own, 0.0, -1e30).astype(np.float32)
    past = np.where(nn < own, 1.0, 0.0).astype(np.float32)
    _CONST["c_negm"] = np.ascontiguousarray(np.broadcast_to(negm[None], (128, 16, 8)))
    _CONST["c_past"] = np.ascontiguousarray(np.broadcast_to(past[None], (128, 16, 8)))
    a = np.arange(64)
    mL = (a[:, None] > a[None, :]).astype(np.float32)
    mU = (a[None, :] > a[:, None]).astype(np.float32)
    mUi = (a[None, :] >= a[:, None]).astype(np.float32)
    for nm, m in (("c_mL", mL), ("c_mU", mU), ("c_mUi", mUi)):
        _CONST[nm] = np.ascontiguousarray(np.broadcast_to(m[:, None, :], (64, 8, 64)))
    rst = np.ones((64, 8, SW), np.float32)
    rst[:, :, 0::CH] = 0.0
    _CONST["c_rst"] = rst
    return _CONST


def _hd(v):
    return np.ascontiguousarray(np.asarray(v).reshape(8, 64).T)


def _prep_inputs(inputs, cfg):
    f = lambda a: np.ascontiguousarray(np.asarray(a, dtype=np.float32))
    x = f(inputs["x"])
    c = f(inputs["c"])
    g = {k: f(v) for k, v in inputs.items()}
    rw64 = []
    for l in range(DEPTH):
        mu = g["rwkv_mu"][l]
        parts = [_hd(g["rwkv_w0"][l]), _hd(g["rwkv_a0"][l]), _hd(g["rwkv_k_k"][l]), _hd(g["rwkv_k_a"][l]),
                 _hd(g["rwkv_r_k"][l].reshape(-1)), _hd(g["rwkv_ln_w"][l]), _hd(g["rwkv_ln_b"][l]),
                 _hd(mu[0:512]), _hd(mu[512:1024]), _hd(mu[1024:1536]),
                 mu[1536:1600].reshape(64, 1), mu[1600:1664].reshape(64, 1)]
        rw64.append(np.concatenate(parts, axis=1))
    cw = g["conv_w"]
    shared = {
        "w_mod": g["w_mod"],
        "b_modc": np.stack([_col(g["b_mod"][l]) for l in range(DEPTH)]),
        "norm_mixc": np.stack([_col(g["norm_mix"][l]) for l in range(DEPTH)]),
        "norm_ffnc": np.stack([_col(g["norm_ffn"][l]) for l in range(DEPTH)]),
        "norm_finc": _col(g["norm_final"]),
        "w_in": g["w_in"], "w_out": g["w_out"], "w_up": g["w_up"], "w_down": g["w_down"],
        "conv_wc": np.stack([np.stack([_col(cw[l, j]) for j in range(3)], axis=1) for l in range(DEPTH)]),
        "conv_bc": np.stack([_col(g["conv_b"][l]) for l in range(DEPTH)]),
        "fox_fb": np.ascontiguousarray(g["fox_f_bias"].reshape(DEPTH, 4, 1)),
        "rw64": np.ascontiguousarray(np.stack(rw64)),
        "rw_mug": np.ascontiguousarray(g["rwkv_mu"][:, 1664:1792].reshape(DEPTH, 128, 1)),
        "rwkv_w2": g["rwkv_w2"], "rwkv_a2": g["rwkv_a2"], "rwkv_g2": g["rwkv_g2"],
    }
    shared.update(_consts())
    in_maps = []
    for b in range(8):
        m = dict(shared)
        m["xT"] = np.ascontiguousarray(x[b].T)
        m["cT"] = _col(c[b])
        in_maps.append(m)
    return in_maps


def run(inputs, cfg):
    key = tuple(sorted(cfg.items()))
    if key not in _CACHE:
        _CACHE[key] = build(cfg)
    nc = _CACHE[key]
    in_maps = _prep_inputs(inputs, cfg)
    res = run_bass_kernel_spmd(nc, in_maps, core_ids=list(range(8)))
    return res


def kernel(**inputs):
    res = run(inputs, {})
    out = np.stack([np.ascontiguousarray(np.asarray(r["outT"]).T) for r in res.results], axis=0)
    return out.astype(np.float32)
```

```python
import contextlib
import numpy as np
import ml_dtypes
import concourse.bass as bass
import concourse.mybir as mybir
from concourse.bass_utils import run_bass_kernel_spmd

F32 = mybir.dt.float32
BF16 = mybir.dt.bfloat16
AF = mybir.ActivationFunctionType
ALU = mybir.AluOpType
AX = mybir.AxisListType

D = 1024
S = 2048
DEPTH = 2
DIN = 3332
DFF = 2816
NEG = -30000.0
EPOCH = 16000


class T:
    __slots__ = ("name", "w", "r")

    def __init__(self, name):
        self.name = name
        self.w = None
        self.r = []


class Prog:
    ENG = ("pe", "act", "dve", "pool", "sp")

    def __init__(self, nc, stack):
        self.nc = nc
        self.q = {e: [] for e in self.ENG}
        self.cnt = {e: 0 for e in self.ENG}
        self.seen = {e: {} for e in self.ENG}
        self.esems = {e: [] for e in self.ENG}
        self.stack = stack
        self.dma_pool = [stack.enter_context(nc.semaphore("dq%d" % i)) for i in range(48)]
        self.dma_cnt = [0] * len(self.dma_pool)
        self.dma_next = {"hw": 0, "sw": 32}
        self.dma_rng = {"hw": (0, 32), "sw": (32, 48)}
        self.dma_last = {}
        self.nsem = 0
        self.capture = None

    def _esem(self, eng, epoch):
        lst = self.esems[eng]
        while len(lst) <= epoch:
            lst.append(self.stack.enter_context(self.nc.semaphore("e_%s_%d" % (eng, len(lst)))))
        return lst[epoch]

    def _need(self, eng, stamp, waits, raw=True, isdma=False):
        if stamp is None:
            return
        sem, val, seng = stamp
        if seng == eng and (eng == "pe" or not raw) and not isdma:
            return
        k = id(sem)
        if self.seen[eng].get(k, 0) >= val:
            return
        self.seen[eng][k] = val
        waits.append((sem, val))

    def _deps(self, eng, reads, writes, isdma=False):
        waits = []
        for t in reads:
            self._need(eng, t.w, waits, isdma=isdma)
        for t in writes:
            self._need(eng, t.w, waits, raw=False, isdma=isdma)
            for st in t.r:
                self._need(eng, st, waits, raw=False, isdma=isdma)
        return waits

    def op(self, eng, fn, reads=(), writes=()):
        if self.capture is not None:
            self.capture.append(("op", eng, fn, tuple(reads), tuple(writes), None))
            return None
        waits = self._deps(eng, reads, writes)
        idx = self.cnt[eng]
        self.cnt[eng] += 1
        sem = self._esem(eng, idx // EPOCH)
        val = idx % EPOCH + 1
        stamp = (sem, val, eng)
        self.q[eng].append((waits, fn, sem, 1))
        for t in reads:
            t.r.append(stamp)
        for t in writes:
            t.w = stamp
            t.r = []
        return stamp

    def dma(self, eng, out, in_, reads=(), writes=(), **kw):
        if self.capture is not None:
            self.capture.append(("dma", eng, (out, in_), tuple(reads), tuple(writes), kw))
            return None
        waits = self._deps(eng, reads, writes, isdma=True)
        cls = "sw" if eng == "pool" else "hw"
        lo, hi = self.dma_rng[cls]
        i = self.dma_next[cls]
        self.dma_next[cls] = lo + (i + 1 - lo) % (hi - lo)
        sem = self.dma_pool[i]
        prev = self.dma_last.get(i)
        if prev is not None:
            self._need(eng, prev, waits)
        self.dma_cnt[i] += 16
        stamp = (sem, self.dma_cnt[i], 'dma')
        self.dma_last[i] = stamp
        self.q[eng].append((waits, lambda e: e.dma_start(out=out, in_=in_, **kw), sem, 16))
        for t in reads:
            t.r.append(stamp)
        for t in writes:
            t.w = stamp
            t.r = []
        return stamp

    def begin_capture(self):
        self.capture = []

    def end_capture(self):
        lst = self.capture
        self.capture = None
        return lst

    def _play1(self, it):
        kind, eng, a, reads, writes, kw = it
        if kind == "op":
            self.op(eng, a, reads, writes)
        else:
            self.dma(eng, a[0], a[1], reads, writes, **kw)

    def play(self, la, lb=()):
        na, nb = len(la), len(lb)
        ia = ib = 0
        while ia < na or ib < nb:
            if ib >= nb or (ia < na and ia * nb <= ib * na):
                self._play1(la[ia])
                ia += 1
            else:
                self._play1(lb[ib])
                ib += 1

    def barrier(self):
        stamps = []
        for e in self.ENG:
            if self.cnt[e] > 0:
                idx = self.cnt[e] - 1
                stamps.append((self._esem(e, idx // EPOCH), idx % EPOCH + 1, 'bar'))
        for i, st in self.dma_last.items():
            stamps.append(st)
        for e in self.ENG:
            waits = []
            for st in stamps:
                self._need(e, st, waits)
            if waits:
                self.q[e].append((waits, None, None, 0))

    def emit(self):
        nc = self.nc
        engs = {"pe": "tensor", "act": "scalar", "dve": "vector", "pool": "gpsimd", "sp": "sync"}
        with nc.Block() as block:
            for ename, attr in engs.items():
                items = self.q[ename]

                def body(e, items=items):
                    for waits, fn, sem, inc in items:
                        for (s, v) in waits:
                            e.wait_ge(s, v)
                        if fn is not None:
                            fn(e).then_inc(sem, inc)

                getattr(block, attr)(body)


def _col(v):
    v = np.asarray(v)
    n = v.shape[-1] // 128
    return np.ascontiguousarray(v.reshape(n, 128).T)


RW0 = 772
MB0 = 2564
CDEC = 0.6065306597126334
SW = 128
CH = 64


def build(cfg):
    nc = bass.Bass("TRN2", target_bir_lowering=False)
    dbg = cfg.get("debug")
    nlayers = cfg.get("nlayers", DEPTH)
    skip = cfg.get("skip", "")

    def din(name, shape, dt=F32):
        return nc.dram_tensor(name, list(shape), dt, kind="ExternalInput").ap()

    xT_d = din("xT", [D, S])
    cT_d = din("cT", [128, 8])
    wmod_d = din("w_mod", [DEPTH, D, 6 * D])
    bmod_d = din("b_modc", [DEPTH, 128, 48])
    nmix_d = din("norm_mixc", [DEPTH, 128, 8])
    nffn_d = din("norm_ffnc", [DEPTH, 128, 8])
    nfin_d = din("norm_finc", [128, 8])
    win_d = din("w_in", [DEPTH, D, DIN])
    wout_d = din("w_out", [DEPTH, D, D])
    wup_d = din("w_up", [DEPTH, D, 2 * DFF])
    wdn_d = din("w_down", [DEPTH, DFF, D])
    cw_d = din("conv_wc", [DEPTH, 128, 3, 44])
    cb_d = din("conv_bc", [DEPTH, 128, 44])
    fb_d = din("fox_fb", [DEPTH, 4, 1])
    r64_d = din("rw64", [DEPTH, 64, 8 * 7 + 24 + 2])
    mug_d = din("rw_mug", [DEPTH, 128, 1])
    w2_d = din("rwkv_w2", [DEPTH, 64, 512])
    a2_d = din("rwkv_a2", [DEPTH, 64, 512])
    g2_d = din("rwkv_g2", [DEPTH, 128, 512])
    c_cb = din("c_cb", [128, 4, 512], BF16)
    c_idb = din("c_idb", [128, 128], BF16)
    c_idf = din("c_idf", [128, 128], F32)
    c_sel = din("c_sel", [64, 4, 4], BF16)
    c_rows = din("c_rows", [12, S], BF16)
    c_oneh = din("c_oneh", [8, S], BF16)
    c_negm = din("c_negm", [128, 16, 8], F32)
    c_past = din("c_past", [128, 16, 8], F32)
    c_mL = din("c_mL", [64, 8, 64], F32)
    c_mU = din("c_mU", [64, 8, 64], F32)
    c_mUi = din("c_mUi", [64, 8, 64], F32)
    c_rst = din("c_rst", [64, 8, SW], F32)

    out_d = nc.dram_tensor("outT", [D, S], F32, kind="ExternalOutput").ap()
    pT_d = nc.dram_tensor("pT", [DIN, S], F32).ap()
    yT_d = nc.dram_tensor("yT", [D, S], BF16).ap()
    zT_d = nc.dram_tensor("zT", [DFF, S], BF16).ap()
    xs_d = nc.dram_tensor("xs", [7, 64, S], F32).ap()
    dbg_d = None
    if dbg in ("yT",):
        dbg_d = nc.dram_tensor("dbg", [D, S], BF16, kind="ExternalOutput").ap()
    if dbg == "rw":
        dbg_d = nc.dram_tensor("dbg", [12, 64, 8, SW], F32, kind="ExternalOutput").ap()
        dbg2_d = nc.dram_tensor("dbg2", [16, 64, 8, 64], F32, kind="ExternalOutput").ap()
    if dbg in ("xm", "xf"):
        dbg_d = nc.dram_tensor("dbg", [D, S], F32, kind="ExternalOutput").ap()

    with contextlib.ExitStack() as stack:
        P = Prog(nc, stack)

        uid = [0]

        def sb(name, shape, dt=F32, st=stack):
            uid[0] += 1
            return st.enter_context(nc.sbuf_tensor("s%d_%s" % (uid[0], name), list(shape), dt))

        def ps(name, shape, dt=F32, st=stack):
            uid[0] += 1
            return st.enter_context(nc.psum_tensor("p%d_%s" % (uid[0], name), list(shape), dt))

        dq = ["sp", "act", "pool"]

        xT = sb("xT", [128, 8, S])
        xT_t = [[T("xT%d_%d" % (j, c)) for c in range(4)] for j in range(8)]
        ones_bf = sb("ones_bf", [128, 128], BF16)
        ones_f = sb("ones_f", [64, 64], F32)
        mean_f = sb("mean_f", [64, 64], F32)
        idb = sb("idb", [128, 128], BF16)
        idf = sb("idf", [128, 128], F32)
        cbias = sb("cbias", [128, 4, 512], BF16)
        ones_t = T("ones")
        eps_col = sb("eps_col", [128, 8])
        eps_t = T("eps")
        cact = sb("cact", [128, 8])
        cact_t = T("cact")
        modT = sb("modT", [128, 48])
        mod_t = T("modT")
        gm = sb("gm", [128, 16])
        gm_t = T("gm")
        bmod = sb("bmod", [128, 48])
        bmod_t = T("bmod")
        nmix = sb("nmix", [128, 16])
        nmix_t = T("nmix")
        nfin = sb("nfin", [128, 8])
        nfin_t = T("nfin")

        P.op("pool", lambda e: e.memset(ones_bf[:], 1.0), writes=[ones_t])
        P.op("pool", lambda e: e.memset(ones_f[:], 1.0), writes=[ones_t])
        P.op("pool", lambda e: e.memset(mean_f[:], 1.0 / 64.0), writes=[ones_t])
        P.op("pool", lambda e: e.memset(eps_col[:, 0:1], 1e-6), writes=[eps_t])
        P.op("pool", lambda e: e.memset(eps_col[:, 1:2], 64e-5), writes=[eps_t])
        P.op("pool", lambda e: e.memset(eps_col[:, 2:3], 1.0), writes=[eps_t])
        P.op("pool", lambda e: e.memset(eps_col[:, 3:4], 0.0), writes=[eps_t])
        P.op("pool", lambda e: e.memset(eps_col[:, 4:5], 1e-18), writes=[eps_t])
        P.dma("pool", idb[:], c_idb[:, :], writes=[ones_t])
        P.dma("pool", idf[:], c_idf[:, :], writes=[ones_t])
        P.dma("pool", cbias[:], c_cb[:, :, :], writes=[ones_t])
        P.dma("pool", nfin[:], nfin_d[:, :], writes=[nfin_t])
        for j in range(8):
            P.dma(dq[j % 2], xT[:, j, :], xT_d[j * 128:(j + 1) * 128, :], writes=xT_t[j])
        P.dma("pool", cact[:], cT_d[:, :], writes=[cact_t])
        P.op("act", lambda e: e.activation(out=cact[:], in_=cact[:], func=AF.Silu),
             reads=[cact_t], writes=[cact_t])
        P.barrier()

        def norm_stage(gcol, shcol, par_t, sink):
            with contextlib.ExitStack() as st1:
                sq = [sb("sq%d" % i, [128, 512], BF16, st1) for i in range(3)]
                sq_t = [T("sq%d" % i) for i in range(3)]
                pss = [ps("pss%d" % i, [128, 512], F32, st1) for i in range(2)]
                pss_t = [T("pss%d" % i) for i in range(2)]
                rstd = [sb("rstd%d" % i, [128, 512], F32, st1) for i in range(2)]
                rstd_t = [T("rstd%d" % i) for i in range(2)]
                tmp = [sb("ntmp%d" % i, [128, 512], F32, st1) for i in range(3)]
                tmp_t = [T("ntmp%d" % i) for i in range(3)]
                n = 0
                for c in range(4):
                    cs = slice(c * 512, (c + 1) * 512)
                    pb = c % 2
                    for j in range(8):
                        i = n % 3
                        n += 1
                        P.op("act", lambda e, i=i, j=j, cs=cs: e.activation(
                            out=sq[i][:], in_=xT[:, j, cs], func=AF.Square),
                            reads=[xT_t[j][c]], writes=[sq_t[i]])
                        P.op("pe", lambda e, i=i, j=j, pb=pb: e.matmul(
                            pss[pb][:], lhsT=ones_bf[:], rhs=sq[i][:], start=(j == 0), stop=(j == 7)),
                            reads=[sq_t[i], ones_t], writes=[pss_t[pb]])
                    P.op("act", lambda e, pb=pb: e.activation(
                        out=rstd[pb][:], in_=pss[pb][:], func=AF.Ln, bias=eps_col[:, 0:1], scale=1.0 / D),
                        reads=[pss_t[pb], eps_t], writes=[rstd_t[pb]])
                    P.op("act", lambda e, pb=pb: e.activation(out=rstd[pb][:], in_=rstd[pb][:], func=AF.Exp, scale=-0.5),
                         reads=[rstd_t[pb]], writes=[rstd_t[pb]])
                    for j in range(8):
                        i = n % 3
                        n += 1
                        P.op("dve", lambda e, i=i, j=j, cs=cs, pb=pb: e.tensor_tensor(
                            out=tmp[i][:], in0=xT[:, j, cs], in1=rstd[pb][:], op=ALU.mult),
                            reads=[xT_t[j][c], rstd_t[pb]], writes=[tmp_t[i]])
                        sink(j, c, cs, tmp[i], tmp_t[i], gcol(j), shcol(j), par_t)
                P.barrier()

        def layer_body(l):
            with contextlib.ExitStack() as st0:
                wm = [sb("wm%d" % i, [128, 8, 512], F32, st0) for i in range(2)]
                wm_t = [T("wm%d" % i) for i in range(2)]
                ps_mod_full = ps("ps_mod", [128, 512], F32, st0)
                ps_mod = ps_mod_full[:, 0:48]
                psm_t = T("ps_mod")
                P.dma("pool", bmod[:], bmod_d[l], writes=[bmod_t])
                P.dma("pool", nmix[:, 0:8], nmix_d[l], writes=[nmix_t])
                P.dma("pool", nmix[:, 8:16], nffn_d[l], writes=[nmix_t])
                wv = wmod_d[l].rearrange("(k p) n -> p k n", p=128)
                for nb in range(12):
                    b = nb % 2
                    P.dma(dq[nb % 2], wm[b][:], wv[:, :, nb * 512:(nb + 1) * 512], writes=[wm_t[b]])
                    for jj in range(4):
                        j = nb * 4 + jj
                        for k in range(8):
                            P.op("pe", lambda e, b=b, jj=jj, j=j, k=k: e.matmul(
                                ps_mod[:, j:j + 1], lhsT=wm[b][:, k, jj * 128:(jj + 1) * 128],
                                rhs=cact[:, k:k + 1], start=(k == 0), stop=(k == 7)),
                                reads=[wm_t[b], cact_t], writes=[psm_t])
                P.op("dve", lambda e: e.tensor_tensor(out=modT[:], in0=ps_mod, in1=bmod[:], op=ALU.add),
                     reads=[psm_t, bmod_t], writes=[mod_t])
                P.op("dve", lambda e: e.scalar_tensor_tensor(
                    out=gm[:, 0:8], in0=modT[:, 8:16], scalar=1.0, in1=nmix[:, 0:8], op0=ALU.add, op1=ALU.mult),
                    reads=[mod_t, nmix_t], writes=[gm_t])
                P.op("dve", lambda e: e.scalar_tensor_tensor(
                    out=gm[:, 8:16], in0=modT[:, 32:40], scalar=1.0, in1=nmix[:, 8:16], op0=ALU.add, op1=ALU.mult),
                    reads=[mod_t, nmix_t], writes=[gm_t])
                P.barrier()

            with contextlib.ExitStack() as stL:
                vtm = [sb("vtm%d" % i, [128, 16, 256], BF16, stL) for i in range(2)]
                vtm_t = [T("vtm%d" % i) for i in range(2)]

                with contextlib.ExitStack() as stH:
                    hT = sb("hT", [128, 8, S], BF16, stH)
                    hT_t = [[T("hT%d_%d" % (j, c)) for c in range(4)] for j in range(8)]

                    def sink_h(j, c, cs, tp, tp_t, g, sh, par_t):
                        P.op("act", lambda e: e.activation(out=hT[:, j, cs], in_=tp[:], func=AF.Identity,
                                                           bias=sh, scale=g),
                             reads=[tp_t] + par_t, writes=[hT_t[j][c]])

                    norm_stage(lambda j: gm[:, j:j + 1], lambda j: modT[:, j:j + 1], [gm_t, mod_t], sink_h)

                    with contextlib.ExitStack() as st2:
                        wst = [sb("wst%d" % i, [128, 8, 256], F32, st2) for i in range(2)]
                        wst_t = [T("wst%d" % i) for i in range(2)]
                        wbf = [sb("wbf%d" % i, [128, 8, 256], BF16, st2) for i in range(2)]
                        wbf_t = [T("wbf%d" % i) for i in range(2)]
                        pp = [ps("pp%d" % i, [128, S], F32, st2) for i in range(2)]
                        pp_t = [[T("pp%d_%d" % (i, c)) for c in range(4)] for i in range(2)]
                        ob = [sb("ob%d" % i, [128, S], F32, st2) for i in range(2)]
                        ob_t = [T("ob%d" % i) for i in range(2)]
                        wiv = win_d[l].rearrange("(k p) n -> p k n", p=128)
                        segs = [(0, 512), (768, 4), (RW0, 1792), (MB0, 512)]
                        tiles = []
                        for (s0, ln) in segs:
                            o = 0
                            while o < ln:
                                m = min(128, ln - o)
                                tiles.append((s0 + o, m))
                                o += m
                        it = 0
                        for (n0, m) in tiles:
                            b = it % 2
                            it += 1
                            P.dma("sp", wst[b][:, :, 0:m], wiv[:, :, n0:n0 + m], writes=[wst_t[b]])
                            P.op("pool", lambda e, b=b, m=m: e.tensor_copy(out=wbf[b][:, :, 0:m], in_=wst[b][:, :, 0:m]),
                                 reads=[wst_t[b]], writes=[wbf_t[b]])
                            for c in range(4):
                                cs = slice(c * 512, (c + 1) * 512)
                                for k in range(8):
                                    P.op("pe", lambda e, b=b, m=m, k=k, cs=cs: e.matmul(
                                        pp[b][0:m, cs], lhsT=wbf[b][:, k, 0:m], rhs=hT[:, k, cs],
                                        start=(k == 0), stop=(k == 7)),
                                        reads=[wbf_t[b], hT_t[k][c]], writes=[pp_t[b][c]])
                                if c % 2 == 0:
                                    P.op("act", lambda e, b=b, m=m, cs=cs: e.activation(
                                        out=ob[b][0:m, cs], in_=pp[b][0:m, cs], func=AF.Identity),
                                        reads=[pp_t[b][c]], writes=[ob_t[b]])
                                else:
                                    P.op("dve", lambda e, b=b, m=m, cs=cs: e.tensor_copy(
                                        out=ob[b][0:m, cs], in_=pp[b][0:m, cs]),
                                        reads=[pp_t[b][c]], writes=[ob_t[b]])
                            P.dma("act", pT_d[n0:n0 + m, :], ob[b][0:m, :], reads=[ob_t[b]])
                        for vi, v0 in enumerate((512, MB0 + 512)):
                            b = it % 2
                            it += 1
                            P.dma("sp", wst[b][:, :, 0:256], wiv[:, :, v0:v0 + 256], writes=[wst_t[b]])
                            P.op("pool", lambda e, b=b: e.tensor_copy(out=wbf[b][:, :, :], in_=wst[b][:, :, :]),
                                 reads=[wst_t[b]], writes=[wbf_t[b]])
                            for tt in range(16):
                                pi = tt % 2
                                c = tt // 4
                                for k in range(8):
                                    P.op("pe", lambda e, b=b, k=k, tt=tt, pi=pi: e.matmul(
                                        pp[pi][:, 0:256], lhsT=hT[:, k, tt * 128:(tt + 1) * 128], rhs=wbf[b][:, k, :],
                                        start=(k == 0), stop=(k == 7)),
                                        reads=[wbf_t[b], hT_t[k][c]], writes=[pp_t[pi][0]])
                                P.op("act" if tt % 2 == 0 else "dve",
                                     (lambda e, vi=vi, tt=tt, pi=pi: e.activation(
                                         out=vtm[vi][:, tt, :], in_=pp[pi][:, 0:256], func=AF.Identity)) if tt % 2 == 0 else
                                     (lambda e, vi=vi, tt=tt, pi=pi: e.tensor_copy(out=vtm[vi][:, tt, :], in_=pp[pi][:, 0:256])),
                                     reads=[pp_t[pi][0]], writes=[vtm_t[vi]])
                        P.barrier()

                def attention(kind):
                    fox = (kind == "fox")
                    KA = 71 if fox else 73
                    qrow0 = 0 if fox else MB0
                    krow0 = 256 if fox else MB0 + 256
                    yrow0 = 0 if fox else 768
                    vt, vt_t = (vtm[0], vtm_t[0]) if fox else (vtm[1], vtm_t[1])
                    with contextlib.ExitStack() as sa:
                        Qh = [sb("Qh%d" % h, [80, S], BF16, sa) for h in range(4)]
                        Kh = [sb("Kh%d" % h, [80, S], BF16, sa) for h in range(4)]
                        Qh_t = [T("Qh%d" % h) for h in range(4)]
                        Kh_t = [T("Kh%d" % h) for h in range(4)]
                        with contextlib.ExitStack() as sp_:
                            stg = [sb("stg%d" % i, [64, S], F32, sp_) for i in range(2)]
                            stg_t = [T("stg%d" % i) for i in range(2)]
                            sqq = [sb("sqq%d" % i, [64, S], BF16, sp_) for i in range(2)]
                            sqq_t = [T("sqq%d" % i) for i in range(2)]
                            selb = sb("selb", [64, 4, 4], BF16, sp_)
                            sel_t = T("selb")
                            P.dma("pool", selb[:], c_sel[:, :, :], writes=[sel_t])
                            fA = sb("fA", [4, S], F32, sp_)
                            fA_t = T("fA")
                            mrow = sb("mrow", [4, S], BF16, sp_)
                            mrow_t = T("mrow")
                            kmax = sb("kmax", [4, 2], F32, sp_)
                            kmax_t = T("kmax")
                            psn = ps("psn", [4, S], F32, sp_)
                            psn_t = T("psn")
                            if fox:
                                fB = sb("fB", [4, S], F32, sp_)
                                fB_t = T("fB")
                                g3 = sb("g3", [4, 3, S], BF16, sp_)
                                g3_t = T("g3")
                                fbb = sb("fbb", [4, 2], F32, sp_)
                                fbb_t = T("fbb")
                            else:
                                kmT = sb("kmT", [64, 4, 8], F32, sp_)
                                kmT_t = T("kmT")
                                psg_full = ps("psg", [128, 64, 8], F32, sp_)
                                psg = psg_full[:, 0:16, :]
                                psg_t = T("psg")
                                psT = [ps("psT%d" % i, [8, 512], F32, sp_) for i in range(2)]
                                psT_t = [T("psT%d" % i) for i in range(2)]
                                gmk = sb("gmk", [128, 16, 8], F32, sp_)
                                gmk_t = T("gmk")
                                top8 = sb("top8", [128, 16, 8], F32, sp_)
                                top8_t = T("top8")
                                negm = sb("negm", [128, 16, 8], F32, sp_)
                                past = sb("past", [128, 16, 8], F32, sp_)
                                cm_t = T("cmask")
                                P.dma("pool", negm[:], c_negm[:, :, :], writes=[cm_t])
                                P.dma("pool", past[:], c_past[:, :, :], writes=[cm_t])
                                mbT = [sb("mbT%d" % i, [8, S], BF16, sp_) for i in range(2)]
                                mbT_t = [T("mbT%d" % i) for i in range(2)]

                            for h in range(4):
                                b = h % 2
                                P.dma(dq[h % 2], stg[b][:], pT_d[krow0 + h * 64: krow0 + (h + 1) * 64, :], writes=[stg_t[b]])
                                P.op("dve", lambda e, h=h, b=b: e.tensor_copy(out=Kh[h][0:64, :], in_=stg[b][:]),
                                     reads=[stg_t[b]], writes=[Kh_t[h]])
                                P.op("act", lambda e, b=b: e.activation(out=sqq[b][:], in_=stg[b][:], func=AF.Square),
                                     reads=[stg_t[b]], writes=[sqq_t[b]])
                                for c in range(4):
                                    cs = slice(c * 512, (c + 1) * 512)
                                    P.op("pe", lambda e, h=h, b=b, cs=cs: e.matmul(
                                        psn[:, cs], lhsT=selb[:, h, :], rhs=sqq[b][:, cs], start=(h == 0), stop=(h == 3)),
                                        reads=[sqq_t[b], sel_t], writes=[psn_t])
                                if not fox:
                                    P.op("dve", lambda e, h=h, b=b: e.tensor_reduce(
                                        out=kmT[:, h, :], in_=stg[b][:].rearrange("p (n s) -> p n s", s=256),
                                        axis=AX.X, op=ALU.add), reads=[stg_t[b]], writes=[kmT_t])
                            if not fox:
                                P.op("dve", lambda e: e.tensor_scalar(out=kmT[:], in0=kmT[:], scalar1=1.0 / 256.0, scalar2=None,
                                                                      op0=ALU.mult), reads=[kmT_t], writes=[kmT_t])
                            P.op("dve", lambda e: e.tensor_reduce(out=kmax[:, 0:1], in_=psn[:, :], axis=AX.X, op=ALU.max),
                                 reads=[psn_t], writes=[kmax_t])
                            P.op("dve", lambda e: e.tensor_scalar(out=kmax[:, 1:2], in0=kmax[:, 0:1], scalar1=1.0 / 64.0,
                                                                  scalar2=None, op0=ALU.mult),
                                 reads=[kmax_t], writes=[kmax_t])
                            for h in range(4):
                                b = h % 2
                                P.dma(dq[h % 2], stg[b][:], pT_d[qrow0 + h * 64: qrow0 + (h + 1) * 64, :], writes=[stg_t[b]])
                                P.op("dve", lambda e, h=h, b=b: e.tensor_scalar(
                                    out=Qh[h][0:64, :], in0=stg[b][:], scalar1=0.125, scalar2=None, op0=ALU.mult),
                                    reads=[stg_t[b]], writes=[Qh_t[h]])
                                P.op("act", lambda e, b=b: e.activation(out=sqq[b][:], in_=stg[b][:], func=AF.Square),
                                     reads=[stg_t[b]], writes=[sqq_t[b]])
                                for c in range(4):
                                    cs = slice(c * 512, (c + 1) * 512)
                                    P.op("pe", lambda e, h=h, b=b, cs=cs: e.matmul(
                                        psn[:, cs], lhsT=selb[:, h, :], rhs=sqq[b][:, cs], start=(h == 0), stop=(h == 3)),
                                        reads=[sqq_t[b], sel_t, kmax_t], writes=[psn_t])
                                if not fox:
                                    for tt in range(16):
                                        P.op("pe", lambda e, h=h, b=b, tt=tt: e.matmul(
                                            psg[:, tt, :], lhsT=stg[b][:, tt * 128:(tt + 1) * 128], rhs=kmT[:, h, :],
                                            start=True, stop=True), reads=[stg_t[b], kmT_t], writes=[psg_t])
                                    P.op("dve", lambda e: e.tensor_tensor(out=gmk[:], in0=psg, in1=negm[:], op=ALU.add),
                                         reads=[psg_t, cm_t], writes=[gmk_t])
                                    for tt in range(16):
                                        P.op("dve", lambda e, tt=tt: e.max(out=top8[:, tt, :], in_=gmk[:, tt, :]),
                                             reads=[gmk_t], writes=[top8_t])
                                    P.op("dve", lambda e: e.tensor_tensor(
                                        out=gmk[:], in0=gmk[:], in1=top8[:, :, 2:3].to_broadcast([128, 16, 8]), op=ALU.is_ge),
                                        reads=[gmk_t, top8_t], writes=[gmk_t])
                                    P.op("dve", lambda e: e.tensor_scalar(
                                        out=gmk[:], in0=gmk[:], scalar1=-1.0, scalar2=-NEG, op0=ALU.add, op1=ALU.mult),
                                        reads=[gmk_t], writes=[gmk_t])
                                    P.op("dve", lambda e: e.tensor_tensor(out=gmk[:], in0=gmk[:], in1=past[:], op=ALU.mult),
                                         reads=[gmk_t, cm_t], writes=[gmk_t])
                                    mb_ = h % 2
                                    for tt in range(16):
                                        pi = (tt // 4) % 2
                                        P.op("pe", lambda e, tt=tt, pi=pi: e.transpose(
                                            psT[pi][:, (tt % 4) * 128:(tt % 4 + 1) * 128], gmk[:, tt, :], idf[:, :]),
                                            reads=[gmk_t, ones_t], writes=[psT_t[pi]])
                                        if tt % 4 == 3:
                                            c = tt // 4
                                            P.op("act", lambda e, pi=pi, c=c, mb_=mb_: e.activation(
                                                out=mbT[mb_][:, c * 512:(c + 1) * 512], in_=psT[pi][:, :], func=AF.Identity),
                                                reads=[psT_t[pi]], writes=[mbT_t[mb_]])
                                    P.dma("sp", Qh[h][64:72, :], mbT[mb_][:, :], reads=[mbT_t[mb_]], writes=[Qh_t[h]])
                            P.op("act", lambda e: e.activation(out=fA[:], in_=psn[:, :], func=AF.Sqrt, scale=kmax[:, 1:2]),
                                 reads=[psn_t, kmax_t], writes=[fA_t])
                            P.op("dve", lambda e: e.tensor_scalar(out=mrow[:], in0=fA[:], scalar1=-1.0, scalar2=None,
                                                                  op0=ALU.mult), reads=[fA_t], writes=[mrow_t])
                            mr = 70 if fox else 72
                            for h in range(4):
                                P.dma(dq[h % 3], Qh[h][mr:mr + 1, :], mrow[h:h + 1, :], reads=[mrow_t], writes=[Qh_t[h]])
                                P.dma(dq[(h + 1) % 3], Kh[h][mr:mr + 1, :], c_rows[6:7, :], writes=[Kh_t[h]])
                            if fox:
                                P.dma("sp", fA[:], pT_d[768:772, :], writes=[fA_t])
                                P.dma("pool", fbb[:, 0:1], fb_d[l], writes=[fbb_t])
                                P.op("dve", lambda e: e.tensor_scalar(out=fbb[:, 1:2], in0=fbb[:, 0:1], scalar1=-1.0,
                                                                      scalar2=None, op0=ALU.mult),
                                     reads=[fbb_t], writes=[fbb_t])
                                P.op("act", lambda e: e.activation(out=fA[:], in_=fA[:], func=AF.Exp, bias=fbb[:, 1:2], scale=-1.0),
                                     reads=[fA_t, fbb_t], writes=[fA_t])
                                P.op("act", lambda e: e.activation(out=fA[:], in_=fA[:], func=AF.Ln, bias=eps_col[0:4, 2:3], scale=1.0),
                                     reads=[fA_t, eps_t], writes=[fA_t])
                                P.op("dve", lambda e: e.tensor_tensor_scan(
                                    out=fB[:], data0=eps_col[0:4, 2:3].to_broadcast([4, S]), data1=fA[:], initial=0.0,
                                    op0=ALU.mult, op1=ALU.add), reads=[fA_t, eps_t], writes=[fB_t])
                                for i in range(3):
                                    P.op("dve", lambda e, i=i: e.tensor_copy(out=g3[:, i, :], in_=fB[:]),
                                         reads=[fB_t], writes=[g3_t])
                                    if i < 2:
                                        P.op("dve", lambda e, i=i: e.tensor_tensor(out=fB[:], in0=fB[:], in1=g3[:, i, :],
                                                                                   op=ALU.subtract),
                                             reads=[fB_t, g3_t], writes=[fB_t])
                                n = 0
                                for h in range(4):
                                    for i in range(3):
                                        P.dma(dq[n % 3], Qh[h][64 + i:65 + i, :], g3[h:h + 1, i, :], reads=[g3_t], writes=[Qh_t[h]])
                                        P.dma(dq[(n + 1) % 3], Kh[h][67 + i:68 + i, :], g3[h:h + 1, i, :], reads=[g3_t], writes=[Kh_t[h]])
                                        n += 1
                                    P.dma(dq[n % 3], Qh[h][67:70, :], c_rows[0:3, :], writes=[Qh_t[h]])
                                    P.dma(dq[(n + 1) % 3], Kh[h][64:67, :], c_rows[3:6, :], writes=[Kh_t[h]])
                            else:
                                for h in range(4):
                                    P.dma(dq[h % 3], Kh[h][64:72, :], c_oneh[:, :], writes=[Kh_t[h]])
                            P.barrier()

                        with contextlib.ExitStack() as sm:
                            pss = [ps("as%d" % i, [128, 512], F32, sm) for i in range(4)]
                            pss_t = [T("as%d" % i) for i in range(4)]
                            pt = [sb("apt%d" % i, [128, 512], BF16, sm) for i in range(4)]
                            pt_t = [T("apt%d" % i) for i in range(4)]
                            psy = [ps("ay%d" % i, [128, 512], F32, sm) for i in range(2)]
                            psd = [ps("ad%d" % i, [128, 512], F32, sm) for i in range(2)]
                            psy_t = [T("ay%d" % i) for i in range(2)]
                            psd_t = [T("ad%d" % i) for i in range(2)]
                            rec = [sb("arec%d" % i, [128, 512], F32, sm) for i in range(2)]
                            rec_t = [T("arec%d" % i) for i in range(2)]
                            ysb = [sb("aysb%d" % i, [128, 512], F32, sm) for i in range(2)]
                            ysb_t = [T("aysb%d" % i) for i in range(2)]
                            yo = [sb("ayo%d" % i, [128, 512], BF16, sm) for i in range(2)]
                            yo_t = [T("ayo%d" % i) for i in range(2)]
                            items = []
                            grp = 0
                            for pair in range(2):
                                for qc in range(4):
                                    nk = 4 * qc + 4
                                    for hh in range(2):
                                        for kt in range(nk):
                                            items.append((grp, pair, qc, hh, kt, nk, hh == 1 and kt == nk - 1))
                                    grp += 1

                            def emit_qk(idx):
                                grp, pair, qc, hh, kt, nk, lastg = items[idx]
                                i = idx % 4
                                h = 2 * pair + hh
                                qs = slice(qc * 512, (qc + 1) * 512)
                                diag = kt >= 4 * qc
                                P.op("pe", lambda e: e.matmul(
                                    pss[i][:], lhsT=Kh[h][0:KA, kt * 128:(kt + 1) * 128], rhs=Qh[h][0:KA, qs],
                                    start=True, stop=(not diag)),
                                    reads=[Kh_t[h], Qh_t[h]], writes=[pss_t[i]])
                                if diag:
                                    j = kt - 4 * qc
                                    P.op("pe", lambda e: e.matmul(
                                        pss[i][:], lhsT=idb[:, :], rhs=cbias[:, j, :], start=False, stop=True),
                                        reads=[ones_t], writes=[pss_t[i]])

                            def emit_rest(idx):
                                grp, pair, qc, hh, kt, nk, lastg = items[idx]
                                i = idx % 4
                                o = grp % 2
                                h = 2 * pair + hh
                                pb = 64 * hh
                                qs = slice(qc * 512, (qc + 1) * 512)
                                P.op("act", lambda e: e.activation(out=pt[i][:], in_=pss[i][:], func=AF.Exp),
                                     reads=[pss_t[i]], writes=[pt_t[i]])
                                if idx + 3 < len(items):
                                    emit_qk(idx + 3)
                                P.op("pe", lambda e: e.matmul(
                                    psy[o][pb:pb + 64, :], lhsT=vt[:, kt, h * 64:(h + 1) * 64], rhs=pt[i][:],
                                    start=(kt == 0), stop=(kt == nk - 1)),
                                    reads=[pt_t[i], vt_t], writes=[psy_t[o]])
                                P.op("pe", lambda e: e.matmul(
                                    psd[o][pb:pb + 64, :], lhsT=ones_bf[:, 0:64], rhs=pt[i][:],
                                    start=(kt == 0), stop=(kt == nk - 1)),
                                    reads=[pt_t[i], ones_t], writes=[psd_t[o]])
                                if lastg:
                                    P.op("dve", lambda e: e.reciprocal(out=rec[o][:], in_=psd[o][:]),
                                         reads=[psd_t[o]], writes=[rec_t[o]])
                                    P.op("act", lambda e: e.activation(out=ysb[o][:], in_=psy[o][:], func=AF.Identity),
                                         reads=[psy_t[o]], writes=[ysb_t[o]])
                                    P.op("dve", lambda e: e.tensor_tensor(out=yo[o][:], in0=ysb[o][:], in1=rec[o][:], op=ALU.mult),
                                         reads=[ysb_t[o], rec_t[o]], writes=[yo_t[o]])
                                    r0 = yrow0 + pair * 128
                                    P.dma("sp", yT_d[r0:r0 + 128, qs], yo[o][:], reads=[yo_t[o]])

                            emit_qk(0)
                            emit_qk(1)
                            emit_qk(2)
                            for idx in range(len(items)):
                                emit_rest(idx)
                            P.barrier()

                if "fox" not in skip:
                    attention("fox")
                if "moba" not in skip:
                    attention("moba")

            def rwkv():
                with contextlib.ExitStack() as sr:
                    prm = sb("rprm", [64, 82], F32, sr)
                    prm_t = T("rprm")
                    w0c, a0c, kkc, kac, rkc, lnw, lnb = (prm[:, i * 8:(i + 1) * 8] for i in range(7))
                    mu_rkv = prm[:, 56:80]
                    mu_w = prm[:, 80:81]
                    mu_a = prm[:, 81:82]
                    omka = sb("omka", [64, 8], F32, sr)
                    mug = sb("rmug", [128, 1], F32, sr)
                    w2s = sb("w2s", [64, 512], F32, sr)
                    a2s = sb("a2s", [64, 512], F32, sr)
                    g2s = sb("g2s", [128, 512], F32, sr)
                    rst = sb("rst", [64, 8, SW], F32, sr)
                    P.dma("sp", prm[:], r64_d[l], writes=[prm_t])
                    P.dma("act", mug[:], mug_d[l], writes=[prm_t])
                    P.dma("sp", w2s[:], w2_d[l], writes=[prm_t])
                    P.dma("act", a2s[:], a2_d[l], writes=[prm_t])
                    P.dma("pool", g2s[:], g2_d[l], writes=[prm_t])
                    P.dma("sp", rst[:], c_rst[:, :, :], writes=[prm_t])
                    P.op("dve", lambda e: e.tensor_scalar(out=omka[:], in0=kac, scalar1=-1.0, scalar2=1.0,
                                                          op0=ALU.mult, op1=ALU.add), reads=[prm_t], writes=[prm_t])
                    ST = sb("ST", [64, 8, 64], F32, sr)
                    ST_t = T("ST")
                    P.op("pool", lambda e: e.memset(ST[:], 0.0), writes=[ST_t])

                    def arr(name, shape=(64, 8, SW), dt=F32):
                        return sb("r_" + name, list(shape), dt, sr), T("r_" + name)

                    rawR, rawR_t = arr("rawR", (64, 8, SW + 1))
                    rawK, rawK_t = arr("rawK", (64, 8, SW + 1))
                    rawV, rawV_t = arr("rawV", (64, 8, SW + 1))
                    rawW, rawW_t = arr("rawW", (64, SW + 1))
                    rawA, rawA_t = arr("rawA", (64, SW + 1))
                    rawG, rawG_t = arr("rawG", (128, SW + 1))
                    TW, TW_t = arr("TW", (64, SW))
                    AL, AL_t = arr("AL", (64, SW))
                    GL, GL_t = arr("GL", (128, SW))
                    SETN = ["R", "K", "V", "KK", "A", "G", "SG", "CS", "D1", "EW", "EWI", "BON"]

                    def mkset(si):
                        Zs = {}
                        for ni, nm in enumerate(SETN):
                            if si == 0:
                                tt_, t_ = arr(nm)
                                Zs[nm] = (tt_[:], t_)
                            else:
                                j_, hf_ = ni // 2, ni % 2
                                ap_ = xT[0:64, j_, hf_ * 1024:(hf_ + 1) * 1024].rearrange("p (h t) -> p h t", t=SW)
                                Zs[nm] = (ap_, T("r1_" + nm))
                        for nm in ("ABb", "BBb", "KBb", "RBb", "Vb"):
                            tt_, t_ = arr("%s%d" % (nm, si), (64, 8, SW), BF16)
                            Zs[nm] = (tt_[:], t_)
                        return Zs
                    sets = [mkset(0), mkset(1)]
                    YR, YR_t = arr("YR")
                    YO, YO_t = arr("YO", (64, 8, SW), BF16)
                    def v16(a):
                        return a[:].rearrange("p h t -> p (h t)").rearrange("p (g i) -> p g i", i=64)
                    TT, TT_t = arr("TT", (64, 16, 64), BF16)
                    Z_, Z_t = arr("Z", (64, 8, 64), BF16)
                    U_, U_t = arr("U", (64, 8, 64), BF16)
                    STb, STb_t = arr("STb", (64, 8, 64), BF16)
                    D1b, D1b_t = arr("D1b", (64, 8, SW), BF16)
                    YRb, YRb_t = arr("YRb", (64, 8, SW), BF16)
                    mean_b = sb("mean_b", [64, 64], BF16, sr)
                    P.op("pool", lambda e: e.memset(mean_b[:], 1.0 / 64.0), writes=[prm_t])
                    w2b = sb("w2b", [64, 512], BF16, sr)
                    a2b = sb("a2b", [64, 512], BF16, sr)
                    g2b = sb("g2b", [128, 512], BF16, sr)
                    P.op("pool", lambda e: e.tensor_copy(out=w2b[:], in_=w2s[:]), reads=[prm_t], writes=[prm_t])
                    P.op("pool", lambda e: e.tensor_copy(out=a2b[:], in_=a2s[:]), reads=[prm_t], writes=[prm_t])
                    P.op("pool", lambda e: e.tensor_copy(out=g2b[:], in_=g2s[:]), reads=[prm_t], writes=[prm_t])
                    TWb, TWb_t = arr("TWb", (64, SW), BF16)
                    ALb, ALb_t = arr("ALb", (64, SW), BF16)
                    GLb, GLb_t = arr("GLb", (128, SW), BF16)
                    Xs = [arr("X%d" % i, (64, 16, 64), BF16) for i in range(2)]
                    Ys = [arr("Y%d" % i, (64, 16, 64), BF16) for i in range(2)]
                    As = [arr("A%d" % i, (64, 16, 64), BF16) for i in range(2)]
                    Bs = [arr("B%d" % i, (64, 16, 64), BF16) for i in range(2)]
                    mL = sb("mL", [64, 64], F32, sr)
                    mU = sb("mU", [64, 64], F32, sr)
                    mUi = sb("mUi", [64, 64], F32, sr)
                    P.dma("sp", mL[:], c_mL[:, 0, :], writes=[prm_t])
                    P.dma("act", mU[:], c_mU[:, 0, :], writes=[prm_t])
                    P.dma("pool", mUi[:], c_mUi[:, 0, :], writes=[prm_t])

                    def bc16(m):
                        return m[:, :].unsqueeze(1).to_broadcast([64, 16, 64])
                    psbig = ps("rbig", [64, 8, SW], F32, sr)
                    psbig_t = T("rbig")
                    psw = [ps("rw%d" % i, [64, 16, 64], F32, sr) for i in range(2)]
                    psw_t = [T("rw%d" % i) for i in range(2)]
                    psq = [ps("rq%d" % i, [64, 8, 64], F32, sr) for i in range(2)]
                    psq_t = [T("rq%d" % i) for i in range(2)]
                    wn = [0]

                    def nextw():
                        i = wn[0] % 2
                        wn[0] += 1
                        return psw[i], psw_t[i]
                    P.barrier()
                    qn = [0]

                    def nextq():
                        i = qn[0] % 2
                        qn[0] += 1
                        return psq[i], psq_t[i]

                    def bc8(col8, w=SW):
                        return col8.unsqueeze(2).to_broadcast([64, 8, w])

                    def flat(a):
                        return a[:].rearrange("p h t -> p (h t)")

                    def prepA(sc, Z):
                        R_, R_t = Z["R"]; K_, K_t = Z["K"]; V_, V_t = Z["V"]; KK, KK_t = Z["KK"]; A_, A_t = Z["A"]
                        G_, G_t = Z["G"]; SG, SG_t = Z["SG"]; CS, CS_t = Z["CS"]; D1, D1_t = Z["D1"]; EW, EW_t = Z["EW"]
                        EWI, EWI_t = Z["EWI"]; BON, BON_t = Z["BON"]
                        ABb, ABb_t = Z["ABb"]; BBb, BBb_t = Z["BBb"]; KBb, KBb_t = Z["KBb"]; RBb, RBb_t = Z["RBb"]; Vb, Vb_t = Z["Vb"]
                        t0 = sc * SW
                        srcs = [(rawR, rawR_t, RW0, 512, True), (rawK, rawK_t, RW0 + 512, 512, True),
                                (rawV, rawV_t, RW0 + 1024, 512, True), (rawW, rawW_t, RW0 + 1536, 64, False),
                                (rawA, rawA_t, RW0 + 1600, 64, False), (rawG, rawG_t, RW0 + 1664, 128, False)]
                        for qi, (buf, bt, r0, nr, hd) in enumerate(srcs):
                            lo = t0 - 1 if sc > 0 else 0
                            dlo = 0 if sc > 0 else 1
                            src = pT_d[r0:r0 + nr, lo:t0 + SW]
                            if hd:
                                if sc == 0:
                                    P.op("pool", lambda e, buf=buf: e.memset(buf[:, :, 0:1], 0.0), writes=[bt])
                                P.dma("sp", buf[:, :, dlo:SW + 1], src.rearrange("(h d) t -> d h t", d=64), writes=[bt])
                            else:
                                if sc == 0:
                                    P.op("pool", lambda e, buf=buf: e.memset(buf[:, 0:1], 0.0), writes=[bt])
                                P.dma("sp", buf[:, dlo:SW + 1], src, writes=[bt])
                        for ai, (raw, raw_t, dst, dst_t) in enumerate(((rawR, rawR_t, R_, R_t), (rawK, rawK_t, K_, K_t),
                                                                       (rawV, rawV_t, V_, V_t))):
                            eng = "dve" if ai != 1 else "pool"
                            mub = bc8(mu_rkv[:, ai * 8:(ai + 1) * 8])
                            P.op(eng, lambda e, raw=raw, dst=dst: e.tensor_tensor(
                                out=dst[:], in0=raw[:, :, 0:SW], in1=raw[:, :, 1:SW + 1], op=ALU.subtract),
                                reads=[raw_t], writes=[dst_t])
                            P.op(eng, lambda e, dst=dst, mub=mub: e.tensor_tensor(out=dst[:], in0=dst[:], in1=mub, op=ALU.mult),
                                 reads=[dst_t, prm_t], writes=[dst_t])
                            P.op(eng, lambda e, raw=raw, dst=dst: e.tensor_tensor(
                                out=dst[:], in0=dst[:], in1=raw[:, :, 1:SW + 1], op=ALU.add),
                                reads=[dst_t, raw_t], writes=[dst_t])
                        for (raw, raw_t, dst, dst_t, mu) in ((rawW, rawW_t, TW, TW_t, mu_w), (rawA, rawA_t, AL, AL_t, mu_a),
                                                              (rawG, rawG_t, GL, GL_t, mug[:, 0:1])):
                            P.op("dve", lambda e, raw=raw, dst=dst: e.tensor_tensor(
                                out=dst[:], in0=raw[:, 0:SW], in1=raw[:, 1:SW + 1], op=ALU.subtract),
                                reads=[raw_t], writes=[dst_t])
                            P.op("dve", lambda e, raw=raw, dst=dst, mu=mu: e.scalar_tensor_tensor(
                                out=dst[:], in0=dst[:], scalar=mu, in1=raw[:, 1:SW + 1], op0=ALU.mult, op1=ALU.add),
                                reads=[dst_t, raw_t, prm_t], writes=[dst_t])
                        P.op("act", lambda e: e.activation(out=TWb[:], in_=TW[:], func=AF.Tanh), reads=[TW_t], writes=[TWb_t])
                        P.op("act", lambda e: e.activation(out=GLb[:], in_=GL[:], func=AF.Sigmoid), reads=[GL_t], writes=[GLb_t])
                        P.op("pool", lambda e: e.tensor_copy(out=ALb[:], in_=AL[:]), reads=[AL_t], writes=[ALb_t])
                        for h in range(8):
                            P.op("pe", lambda e, h=h: e.matmul(psbig[:, h, :], lhsT=w2b[:, h * 64:(h + 1) * 64], rhs=TWb[:],
                                                               start=True, stop=True),
                                 reads=[TWb_t, prm_t], writes=[psbig_t])
                        P.op("dve", lambda e: e.tensor_tensor(out=SG[:], in0=psbig[:], in1=bc8(w0c), op=ALU.add),
                             reads=[psbig_t, prm_t], writes=[SG_t])
                        P.op("act", lambda e: e.activation(out=SG[:], in_=SG[:], func=AF.Sigmoid), reads=[SG_t], writes=[SG_t])
                        for h in range(8):
                            P.op("pe", lambda e, h=h: e.matmul(psbig[:, h, :], lhsT=a2b[:, h * 64:(h + 1) * 64], rhs=ALb[:],
                                                               start=True, stop=True),
                                 reads=[ALb_t, prm_t], writes=[psbig_t])
                        P.op("dve", lambda e: e.tensor_tensor(out=A_[:], in0=psbig[:], in1=bc8(a0c), op=ALU.add),
                             reads=[psbig_t, prm_t], writes=[A_t])
                        P.op("act", lambda e: e.activation(out=A_[:], in_=A_[:], func=AF.Sigmoid), reads=[A_t], writes=[A_t])
                        for h in range(8):
                            P.op("pe", lambda e, h=h: e.matmul(psbig[:, h, :], lhsT=g2b[:, h * 64:(h + 1) * 64], rhs=GLb[:],
                                                               start=True, stop=True),
                                 reads=[GLb_t, prm_t], writes=[psbig_t])
                        P.op("act", lambda e: e.activation(out=G_[:], in_=psbig[:], func=AF.Identity),
                             reads=[psbig_t], writes=[G_t])
                        P.op("pool", lambda e: e.tensor_tensor(out=KK[:], in0=K_[:], in1=bc8(kkc), op=ALU.mult),
                             reads=[K_t, prm_t], writes=[KK_t])
                        P.op("pool", lambda e: e.tensor_tensor(out=D1b[:], in0=KK[:], in1=KK[:], op=ALU.mult),
                             reads=[KK_t], writes=[D1b_t])
                        for hp in range(2):
                            P.op("pe", lambda e, hp=hp: e.matmul(
                                psbig[:, hp * 4:(hp + 1) * 4, :].rearrange("p h t -> p (h t)"), lhsT=ones_bf[0:64, 0:64],
                                rhs=D1b[:, hp * 4:(hp + 1) * 4, :].rearrange("p h t -> p (h t)"), start=True, stop=True),
                                reads=[D1b_t, ones_t], writes=[psbig_t])
                        P.op("act", lambda e: e.activation(out=D1[:], in_=psbig[:], func=AF.Ln, bias=eps_col[0:64, 4:5], scale=1.0),
                             reads=[psbig_t, eps_t], writes=[D1_t])
                        P.op("act", lambda e: e.activation(out=D1[:], in_=D1[:], func=AF.Exp, scale=-0.5), reads=[D1_t], writes=[D1_t])
                        P.op("dve", lambda e: e.tensor_tensor(out=KK[:], in0=KK[:], in1=D1[:], op=ALU.mult),
                             reads=[KK_t, D1_t], writes=[KK_t])
                        P.op("pool", lambda e: e.tensor_tensor(out=D1[:], in0=A_[:], in1=bc8(kac), op=ALU.mult),
                             reads=[A_t, prm_t], writes=[D1_t])
                        P.op("pool", lambda e: e.tensor_tensor(out=D1[:], in0=D1[:], in1=bc8(omka[:, :]), op=ALU.add),
                             reads=[D1_t, prm_t], writes=[D1_t])
                        P.op("pool", lambda e: e.tensor_tensor(out=K_[:], in0=K_[:], in1=D1[:], op=ALU.mult),
                             reads=[K_t, D1_t], writes=[K_t])
                        P.op("dve", lambda e: e.tensor_tensor(out=D1[:], in0=R_[:], in1=K_[:], op=ALU.mult),
                             reads=[R_t, K_t], writes=[D1_t])
                        P.op("dve", lambda e: e.tensor_tensor(out=D1b[:], in0=D1[:], in1=bc8(rkc), op=ALU.mult),
                             reads=[D1_t, prm_t], writes=[D1b_t])
                        for hp in range(2):
                            P.op("pe", lambda e, hp=hp: e.matmul(
                                psbig[:, hp * 4:(hp + 1) * 4, :].rearrange("p h t -> p (h t)"), lhsT=ones_bf[0:64, 0:64],
                                rhs=D1b[:, hp * 4:(hp + 1) * 4, :].rearrange("p h t -> p (h t)"), start=True, stop=True),
                                reads=[D1b_t, ones_t], writes=[psbig_t])
                        P.op("dve", lambda e: e.tensor_tensor(out=BON[:], in0=psbig[:], in1=V_[:], op=ALU.mult),
                             reads=[psbig_t, V_t], writes=[BON_t])
                        P.op("dve", lambda e: e.tensor_tensor_scan(out=flat(CS), data0=flat(rst), data1=flat(SG), initial=0.0,
                                                                   op0=ALU.mult, op1=ALU.add),
                             reads=[SG_t, prm_t], writes=[CS_t])
                        P.op("act", lambda e: e.activation(out=EW[:], in_=CS[:], func=AF.Exp, scale=-CDEC), reads=[CS_t], writes=[EW_t])
                        P.op("act", lambda e: e.activation(out=EWI[:], in_=CS[:], func=AF.Exp, scale=CDEC), reads=[CS_t], writes=[EWI_t])
                        P.op("dve", lambda e: e.tensor_tensor(out=D1[:], in0=CS[:], in1=SG[:], op=ALU.subtract),
                             reads=[CS_t, SG_t], writes=[D1_t])
                        P.op("act", lambda e: e.activation(out=D1[:], in_=D1[:], func=AF.Exp, scale=-CDEC), reads=[D1_t], writes=[D1_t])
                        P.op("dve", lambda e: e.tensor_tensor(out=R_[:], in0=R_[:], in1=EW[:], op=ALU.mult),
                             reads=[R_t, EW_t], writes=[R_t])
                        P.op("pool", lambda e: e.tensor_tensor(out=K_[:], in0=K_[:], in1=EWI[:], op=ALU.mult),
                             reads=[K_t, EWI_t], writes=[K_t])
                        P.op("dve", lambda e: e.tensor_tensor(out=A_[:], in0=KK[:], in1=A_[:], op=ALU.mult),
                             reads=[KK_t, A_t], writes=[A_t])
                        P.op("dve", lambda e: e.tensor_tensor(out=BBb[:], in0=A_[:], in1=EWI[:], op=ALU.mult),
                             reads=[A_t, EWI_t], writes=[BBb_t])
                        P.op("dve", lambda e: e.scalar_tensor_tensor(out=ABb[:], in0=KK[:], scalar=-1.0, in1=D1[:],
                                                                     op0=ALU.mult, op1=ALU.mult),
                             reads=[KK_t, D1_t], writes=[ABb_t])

                        P.op("act", lambda e: e.activation(out=KBb[:], in_=K_[:], func=AF.Identity), reads=[K_t], writes=[KBb_t])
                        P.op("act", lambda e: e.activation(out=RBb[:], in_=R_[:], func=AF.Identity), reads=[R_t], writes=[RBb_t])
                        P.op("pool", lambda e: e.tensor_copy(out=Vb[:], in_=V_[:]), reads=[V_t], writes=[Vb_t])

                    def runB(sc, Z):
                        R_, R_t = Z["R"]; K_, K_t = Z["K"]; V_, V_t = Z["V"]; KK, KK_t = Z["KK"]; A_, A_t = Z["A"]
                        G_, G_t = Z["G"]; SG, SG_t = Z["SG"]; CS, CS_t = Z["CS"]; D1, D1_t = Z["D1"]; EW, EW_t = Z["EW"]
                        EWI, EWI_t = Z["EWI"]; BON, BON_t = Z["BON"]
                        ABb, ABb_t = Z["ABb"]; BBb, BBb_t = Z["BBb"]; KBb, KBb_t = Z["KBb"]; RBb, RBb_t = Z["RBb"]; Vb, Vb_t = Z["Vb"]
                        t0 = sc * SW
                        def bview(a, half):
                            return a[:].rearrange("p h t -> p (h t)").bitcast(BF16)[:, half * 1024:(half + 1) * 1024].rearrange(
                                "p (g i) -> p g i", i=64)
                        Vt, Vt_t = bview(SG, 0), SG_t
                        KBt, KBt_t = bview(CS, 0), CS_t
                        BBt, BBt_t = bview(EWI, 0), EWI_t
                        Lak, Lak_t = bview(A_, 0), A_t
                        Qrb, Qrb_t = bview(KK, 0), KK_t
                        Qrk, Qrk_t = bview(D1, 0), D1_t
                        NCH = SW // CH

                        def csl_(cc):
                            return slice(cc * CH, (cc + 1) * CH)

                        for (src, src_t, dst, dst_t) in ((Vb, Vb_t, Vt, Vt_t), (KBb, KBb_t, KBt, KBt_t), (BBb, BBb_t, BBt, BBt_t)):
                            pw, pw_t = nextw()
                            pwb = pw[:].rearrange("p g i -> p (g i)").bitcast(BF16)[:, 0:1024].rearrange("p (g i) -> p g i", i=64)
                            for cc in range(NCH):
                                for h in range(8):
                                    P.op("pe", lambda e, pwb=pwb, src=src, h=h, cc=cc: e.transpose(
                                        pwb[:, cc * 8 + h, :], src[:, h, csl_(cc)], idb[0:64, 0:64]),
                                        reads=[src_t, ones_t], writes=[pw_t])
                            P.op("act", lambda e, pwb=pwb, dst=dst: e.activation(out=dst, in_=pwb, func=AF.Identity),
                                 reads=[pw_t], writes=[dst_t])
                        X0, X0_t = Xs[0]
                        Y0, Y0_t = Ys[0]
                        prods = [(ABb, ABb_t, BBb, BBb_t, Y0[:], Y0_t, mL),
                                 (BBb, BBb_t, ABb, ABb_t, X0[:], X0_t, mU),
                                 (KBb, KBb_t, ABb, ABb_t, Lak, Lak_t, mU),
                                 (BBb, BBb_t, RBb, RBb_t, Qrb, Qrb_t, mUi),
                                 (KBb, KBb_t, RBb, RBb_t, Qrk, Qrk_t, mUi)]
                        for pi_, (la, la_t, ra, ra_t, dst, dst_t, msk) in enumerate(prods):
                            pw, pw_t = nextw()
                            for cc in range(NCH):
                                for h in range(8):
                                    P.op("pe", lambda e, pw=pw, la=la, ra=ra, h=h, cc=cc: e.matmul(
                                        pw[:, cc * 8 + h, :], lhsT=la[:, h, csl_(cc)], rhs=ra[:, h, csl_(cc)], start=True, stop=True),
                                        reads=[la_t, ra_t], writes=[pw_t])
                            P.op("dve", lambda e, pw=pw, dst=dst, msk=msk: e.tensor_tensor(
                                out=dst, in0=pw[:], in1=bc16(msk), op=ALU.mult),
                                reads=[pw_t, prm_t], writes=[dst_t])
                        A0, A0_t = As[0]
                        B0, B0_t = Bs[0]
                        P.op("dve", lambda e: e.tensor_tensor(out=A0[:], in0=X0[:], in1=idb[0:64, 0:64].unsqueeze(1).to_broadcast([64, 16, 64]),
                                                              op=ALU.add), reads=[X0_t, ones_t], writes=[A0_t])
                        P.op("pool", lambda e: e.tensor_tensor(out=B0[:], in0=Y0[:], in1=idb[0:64, 0:64].unsqueeze(1).to_broadcast([64, 16, 64]),
                                                               op=ALU.add), reads=[Y0_t, ones_t], writes=[B0_t])
                        for j in range(5):
                            Xc, Xc_t = Xs[j % 2]
                            Yc, Yc_t = Ys[j % 2]
                            Xn, Xn_t = Xs[(j + 1) % 2]
                            Yn, Yn_t = Ys[(j + 1) % 2]
                            Ac, Ac_t = As[j % 2]
                            Bc, Bc_t = Bs[j % 2]
                            An, An_t = As[(j + 1) % 2]
                            Bn, Bn_t = Bs[(j + 1) % 2]
                            pw, pw_t = nextw()
                            for g in range(16):
                                P.op("pe", lambda e, pw=pw, Yc=Yc, Xc=Xc, g=g: e.matmul(
                                    pw[:, g, :], lhsT=Yc[:, g, :], rhs=Xc[:, g, :], start=True, stop=True),
                                    reads=[Yc_t, Xc_t], writes=[pw_t])
                            P.op("act", lambda e, pw=pw, Xn=Xn: e.activation(out=Xn[:], in_=pw[:], func=AF.Identity),
                                 reads=[pw_t], writes=[Xn_t])
                            if j < 4:
                                pw, pw_t = nextw()
                                for g in range(16):
                                    P.op("pe", lambda e, pw=pw, Yc=Yc, Xc=Xc, g=g: e.matmul(
                                        pw[:, g, :], lhsT=Xc[:, g, :], rhs=Yc[:, g, :], start=True, stop=True),
                                        reads=[Yc_t, Xc_t], writes=[pw_t])
                                P.op("act", lambda e, pw=pw, Yn=Yn: e.activation(out=Yn[:], in_=pw[:], func=AF.Identity),
                                     reads=[pw_t], writes=[Yn_t])
                            pw, pw_t = nextw()
                            for g in range(16):
                                P.op("pe", lambda e, pw=pw, Bc=Bc, Xn=Xn, g=g: e.matmul(
                                    pw[:, g, :], lhsT=Bc[:, g, :], rhs=Xn[:, g, :], start=True, stop=False),
                                    reads=[Bc_t, Xn_t], writes=[pw_t])
                                P.op("pe", lambda e, pw=pw, Ac=Ac, g=g: e.matmul(
                                    pw[:, g, :], lhsT=idb[0:64, 0:64], rhs=Ac[:, g, :], start=False, stop=True),
                                    reads=[Ac_t, ones_t], writes=[pw_t])
                            if j < 4:
                                P.op("act", lambda e, pw=pw, An=An: e.activation(out=An[:], in_=pw[:], func=AF.Identity),
                                     reads=[pw_t], writes=[An_t])
                                pw, pw_t = nextw()
                                for g in range(16):
                                    P.op("pe", lambda e, pw=pw, Bc=Bc, Xn=Xn, g=g: e.matmul(
                                        pw[:, g, :], lhsT=Xn[:, g, :], rhs=Bc[:, g, :], start=True, stop=False),
                                        reads=[Bc_t, Xn_t], writes=[pw_t])
                                    P.op("pe", lambda e, pw=pw, Bc=Bc, g=g: e.matmul(
                                        pw[:, g, :], lhsT=idb[0:64, 0:64], rhs=Bc[:, g, :], start=False, stop=True),
                                        reads=[Bc_t, ones_t], writes=[pw_t])
                                P.op("act", lambda e, pw=pw, Bn=Bn: e.activation(out=Bn[:], in_=pw[:], func=AF.Identity),
                                     reads=[pw_t], writes=[Bn_t])
                            else:
                                P.op("act", lambda e, pw=pw: e.activation(out=TT[:], in_=pw[:], func=AF.Identity),
                                     reads=[pw_t], writes=[TT_t])

                        def do_chunk(cc, csl):
                            g0 = cc * 8
                            P.op("pool", lambda e: e.tensor_copy(out=STb[:], in_=ST[:]), reads=[ST_t], writes=[STb_t])
                            pq, pq_t = nextq()
                            for h in range(8):
                                P.op("pe", lambda e, pq=pq, h=h: e.matmul(pq[:, h, :], lhsT=ABb[:, h, csl], rhs=STb[:, h, :],
                                                                          start=True, stop=False),
                                     reads=[ABb_t, STb_t], writes=[pq_t])
                                P.op("pe", lambda e, pq=pq, h=h: e.matmul(pq[:, h, :], lhsT=Lak[:, g0 + h, :], rhs=Vt[:, g0 + h, :],
                                                                          start=False, stop=True),
                                     reads=[Lak_t, Vt_t], writes=[pq_t])
                            P.op("act", lambda e, pq=pq: e.activation(out=Z_[:], in_=pq[:], func=AF.Identity),
                                 reads=[pq_t], writes=[Z_t])
                            pq, pq_t = nextq()
                            for h in range(8):
                                P.op("pe", lambda e, pq=pq, h=h: e.matmul(pq[:, h, :], lhsT=TT[:, g0 + h, :], rhs=Z_[:, h, :],
                                                                          start=True, stop=True),
                                     reads=[TT_t, Z_t], writes=[pq_t])
                            P.op("act", lambda e, pq=pq: e.activation(out=U_[:], in_=pq[:], func=AF.Identity),
                                 reads=[pq_t], writes=[U_t])
                            pq, pq_t = nextq()
                            for h in range(8):
                                P.op("pe", lambda e, pq=pq, h=h: e.matmul(pq[:, h, :], lhsT=STb[:, h, :], rhs=RBb[:, h, csl],
                                                                          start=True, stop=False),
                                     reads=[STb_t, RBb_t], writes=[pq_t])
                                P.op("pe", lambda e, pq=pq, h=h: e.matmul(pq[:, h, :], lhsT=U_[:, h, :], rhs=Qrb[:, g0 + h, :],
                                                                          start=False, stop=False),
                                     reads=[U_t, Qrb_t], writes=[pq_t])
                                P.op("pe", lambda e, pq=pq, h=h: e.matmul(pq[:, h, :], lhsT=Vt[:, g0 + h, :], rhs=Qrk[:, g0 + h, :],
                                                                          start=False, stop=True),
                                     reads=[Vt_t, Qrk_t], writes=[pq_t])
                            P.op("act", lambda e, pq=pq: e.activation(out=YR[:, :, csl], in_=pq[:], func=AF.Identity),
                                 reads=[pq_t], writes=[YR_t])
                            pq, pq_t = nextq()
                            for h in range(8):
                                P.op("pe", lambda e, pq=pq, h=h: e.matmul(pq[:, h, :], lhsT=BBt[:, g0 + h, :], rhs=U_[:, h, :],
                                                                          start=True, stop=False),
                                     reads=[BBt_t, U_t], writes=[pq_t])
                                P.op("pe", lambda e, pq=pq, h=h: e.matmul(pq[:, h, :], lhsT=KBt[:, g0 + h, :], rhs=Vt[:, g0 + h, :],
                                                                          start=False, stop=True),
                                     reads=[KBt_t, Vt_t], writes=[pq_t])
                            P.op("dve", lambda e, pq=pq: e.tensor_tensor(out=ST[:], in0=ST[:], in1=pq[:], op=ALU.add),
                                 reads=[pq_t, ST_t], writes=[ST_t])
                            last = cc * CH + CH - 1
                            P.op("dve", lambda e, last=last: e.tensor_tensor(
                                out=ST[:], in0=ST[:], in1=EW[:, :, last:last + 1].to_broadcast([64, 8, 64]), op=ALU.mult),
                                reads=[ST_t, EW_t], writes=[ST_t])

                        for cc in range(SW // CH):
                            do_chunk(cc, slice(cc * CH, (cc + 1) * CH))

                        def headsum(src, src_t):
                            pw, pw_t = nextw()
                            pwv = pw[:].rearrange("p g i -> p (g i)").rearrange("p (h t) -> p h t", t=SW)
                            for hp in range(2):
                                P.op("pe", lambda e, hp=hp, pwv=pwv: e.matmul(
                                    pwv[:, hp * 4:(hp + 1) * 4, :].rearrange("p h t -> p (h t)"), lhsT=mean_b[:, :],
                                    rhs=src[:, hp * 4:(hp + 1) * 4, :].rearrange("p h t -> p (h t)"), start=True, stop=True),
                                    reads=[src_t, ones_t], writes=[pw_t])
                            return pwv, pw_t
                        P.op("act", lambda e: e.activation(out=YRb[:], in_=YR[:], func=AF.Identity), reads=[YR_t], writes=[YRb_t])
                        pwv, pw_t = headsum(YRb, YRb_t)
                        P.op("dve", lambda e, pwv=pwv: e.tensor_tensor(out=YR[:], in0=YR[:], in1=pwv, op=ALU.subtract),
                             reads=[YR_t, pw_t], writes=[YR_t])
                        P.op("pool", lambda e: e.tensor_tensor(out=D1b[:], in0=YR[:], in1=YR[:], op=ALU.mult),
                             reads=[YR_t], writes=[D1b_t])
                        pwv2, pw2_t = headsum(D1b, D1b_t)
                        P.op("act", lambda e, pwv2=pwv2: e.activation(out=D1[:], in_=pwv2, func=AF.Ln, bias=eps_col[0:64, 1:2], scale=1.0),
                             reads=[pw2_t, eps_t], writes=[D1_t])
                        P.op("act", lambda e: e.activation(out=D1[:], in_=D1[:], func=AF.Exp, scale=-0.5), reads=[D1_t], writes=[D1_t])
                        P.op("dve", lambda e: e.tensor_tensor(out=YR[:], in0=YR[:], in1=D1[:], op=ALU.mult),
                             reads=[YR_t, D1_t], writes=[YR_t])
                        P.op("dve", lambda e: e.tensor_tensor(out=YR[:], in0=YR[:], in1=bc8(lnw), op=ALU.mult),
                             reads=[YR_t, prm_t], writes=[YR_t])
                        P.op("dve", lambda e: e.tensor_tensor(out=YR[:], in0=YR[:], in1=bc8(lnb), op=ALU.add),
                             reads=[YR_t, prm_t], writes=[YR_t])
                        P.op("pool", lambda e: e.tensor_tensor(out=YR[:], in0=YR[:], in1=BON[:], op=ALU.add),
                             reads=[YR_t, BON_t], writes=[YR_t])
                        P.op("pool", lambda e: e.tensor_tensor(out=YO[:], in0=YR[:], in1=G_[:], op=ALU.mult),
                             reads=[YR_t, G_t], writes=[YO_t])
                        P.dma("sp", yT_d[256:768, t0:t0 + SW].rearrange("(h d) t -> d h t", d=64), YO[:], reads=[YO_t])

                    for j in range(7):
                        P.dma(dq[j % 3], xs_d[j], xT[0:64, j, :], reads=xT_t[j])
                    P.barrier()
                    NSC = S // SW
                    P.begin_capture()
                    prepA(0, sets[0])
                    P.play(P.end_capture())
                    for sc in range(NSC):
                        la = []
                        if sc + 1 < NSC:
                            P.begin_capture()
                            prepA(sc + 1, sets[(sc + 1) % 2])
                            la = P.end_capture()
                        P.begin_capture()
                        runB(sc, sets[sc % 2])
                        lb = P.end_capture()
                        ksp = int(len(lb) * 0.30)
                        P.play(lb[:ksp])
                        P.play(lb[ksp:], la)
                    P.barrier()
                    for j in range(7):
                        P.dma(dq[j % 3], xT[0:64, j, :], xs_d[j], writes=xT_t[j])
                    P.barrier()

            if "rwkv" not in skip:
                rwkv()

            if dbg == "rw":
                return True
            if dbg == "yT" and l == nlayers - 1:
                with contextlib.ExitStack() as sd:
                    yb = sb("dbg_y", [128, 8, S], BF16, sd)
                    yb_t = T("dbg_y")
                    P.dma("sp", yb[:], yT_d.rearrange("(j p) t -> p j t", p=128), writes=[yb_t])
                    P.dma("sp", dbg_d.rearrange("(j p) t -> p j t", p=128), yb[:], reads=[yb_t])
                    P.barrier()
                return True

            with contextlib.ExitStack() as so:
                yTs = sb("yTs", [128, 8, S], BF16, so)
                yTs_t = [T("yTs%d" % j) for j in range(8)]
                for j in range(8):
                    P.dma(dq[j % 3], yTs[:, j, :], yT_d[j * 128:(j + 1) * 128, :], writes=[yTs_t[j]])
                owst = [sb("oowst%d" % i, [128, 8, 128], F32, so) for i in range(2)]
                owst_t = [T("oowst%d" % i) for i in range(2)]
                owbf = [sb("oowbf%d" % i, [128, 8, 128], BF16, so) for i in range(2)]
                owbf_t = [T("oowbf%d" % i) for i in range(2)]
                opp = [ps("opp%d" % i, [128, S], F32, so) for i in range(2)]
                opp_t = [[T("opp%d_%d" % (i, c)) for c in range(4)] for i in range(2)]
                wov = wout_d[l].rearrange("(k p) n -> p k n", p=128)
                for m in range(8):
                    b = m % 2
                    P.dma("sp", owst[b][:], wov[:, :, m * 128:(m + 1) * 128], writes=[owst_t[b]])
                    P.op("pool", lambda e, b=b: e.tensor_copy(out=owbf[b][:], in_=owst[b][:]), reads=[owst_t[b]], writes=[owbf_t[b]])
                    for c in range(4):
                        cs = slice(c * 512, (c + 1) * 512)
                        for k in range(8):
                            P.op("pe", lambda e, b=b, k=k, cs=cs: e.matmul(
                                opp[b][:, cs], lhsT=owbf[b][:, k, :], rhs=yTs[:, k, cs], start=(k == 0), stop=(k == 7)),
                                reads=[owbf_t[b], yTs_t[k]], writes=[opp_t[b][c]])
                        P.op("dve", lambda e, b=b, m=m, cs=cs: e.scalar_tensor_tensor(
                            out=xT[:, m, cs], in0=opp[b][:, cs], scalar=modT[:, 16 + m:17 + m], in1=xT[:, m, cs],
                            op0=ALU.mult, op1=ALU.add),
                            reads=[opp_t[b][c], mod_t, xT_t[m][c]], writes=[xT_t[m][c]])
                P.barrier()

            if dbg == "xm" and l == nlayers - 1:
                for j in range(8):
                    P.dma(dq[j % 3], dbg_d[j * 128:(j + 1) * 128, :], xT[:, j, :], reads=xT_t[j])
                P.barrier()
                return True

            with contextlib.ExitStack() as sf:
                h2T = sb("h2T", [128, 8, S], BF16, sf)
                h2T_t = [[T("h2T%d_%d" % (j, c)) for c in range(4)] for j in range(8)]

                def sink_h2(j, c, cs, tp, tp_t, g, sh, par_t):
                    P.op("act", lambda e: e.activation(out=h2T[:, j, cs], in_=tp[:], func=AF.Identity, bias=sh, scale=g),
                         reads=[tp_t] + par_t, writes=[h2T_t[j][c]])

                norm_stage(lambda j: gm[:, 8 + j:9 + j], lambda j: modT[:, 24 + j:25 + j], [gm_t, mod_t], sink_h2)
                with contextlib.ExitStack() as su:
                    cwc = sb("cwc", [128, 3, 44], F32, su)
                    cbc = sb("cbc", [128, 44], F32, su)
                    cw_t = T("cwc")
                    P.dma("pool", cwc[:], cw_d[l], writes=[cw_t])
                    P.dma("pool", cbc[:], cb_d[l], writes=[cw_t])
                    uwst = [sb("uuwst%d" % i, [128, 8, 256], F32, su) for i in range(2)]
                    uwst_t = [T("uuwst%d" % i) for i in range(2)]
                    uwbf = [sb("uuwbf%d" % i, [128, 8, 256], BF16, su) for i in range(2)]
                    uwbf_t = [T("uuwbf%d" % i) for i in range(2)]
                    pu = [[ps("pu%d_%d" % (gv, hf), [128, 1024], F32, su) for hf in range(2)] for gv in range(2)]
                    pu_t = [[[T("pu%d_%d_%d" % (gv, hf, c)) for c in range(2)] for hf in range(2)] for gv in range(2)]
                    ugh = [[sb("ug%d_%d" % (gv, hf), [128, 1026], F32, su) for hf in range(2)] for gv in range(2)]
                    ugh_t = [[T("ug%d_%d" % (gv, hf)) for hf in range(2)] for gv in range(2)]
                    cgh = [[sb("cg%d_%d" % (gv, hf), [128, 1024], F32, su) for hf in range(2)] for gv in range(2)]
                    cgh_t = [[T("cg%d_%d" % (gv, hf)) for hf in range(2)] for gv in range(2)]
                    zo = [sb("zo%d" % i, [128, S], BF16, su) for i in range(2)]
                    zo_t = [[T("zo%d_%d" % (i, hf)) for hf in range(2)] for i in range(2)]
                    for gv in range(2):
                        P.op("pool", lambda e, gv=gv: e.memset(ugh[gv][0][:, 0:2], 0.0), writes=[ugh_t[gv][0]])
                    wuv = wup_d[l].rearrange("(k p) n -> p k n", p=128)

                    def ffn_prefetch(jf):
                        b = jf % 2
                        P.dma("sp", uwst[b][:, :, 0:128], wuv[:, :, jf * 128:(jf + 1) * 128], writes=[uwst_t[b]])
                        P.dma("sp", uwst[b][:, :, 128:256], wuv[:, :, DFF + jf * 128:DFF + (jf + 1) * 128], writes=[uwst_t[b]])
                        P.op("pool", lambda e: e.tensor_copy(out=uwbf[b][:], in_=uwst[b][:]), reads=[uwst_t[b]], writes=[uwbf_t[b]])

                    def ffn_tile(jf):
                        b = jf % 2
                        if jf + 1 < 22:
                            ffn_prefetch(jf + 1)
                        for hf in range(2):
                            for gv in range(2):
                                for c2 in range(2):
                                    c = hf * 2 + c2
                                    cs = slice(c * 512, (c + 1) * 512)
                                    for k in range(8):
                                        P.op("pe", lambda e, gv=gv, hf=hf, c2=c2, k=k, cs=cs: e.matmul(
                                            pu[gv][hf][:, c2 * 512:(c2 + 1) * 512], lhsT=uwbf[b][:, k, gv * 128:(gv + 1) * 128],
                                            rhs=h2T[:, k, cs], start=(k == 0), stop=(k == 7)),
                                            reads=[uwbf_t[b], h2T_t[k][c]], writes=[pu_t[gv][hf][c2]])
                                P.op("act", lambda e, gv=gv, hf=hf: e.activation(
                                    out=ugh[gv][hf][:, 2:1026], in_=pu[gv][hf][:, :], func=AF.Identity),
                                    reads=pu_t[gv][hf], writes=[ugh_t[gv][hf]])
                                if hf == 0:
                                    P.op("act", lambda e, gv=gv: e.activation(
                                        out=ugh[gv][1][:, 0:2], in_=pu[gv][0][:, 1022:1024], func=AF.Identity),
                                        reads=pu_t[gv][0], writes=[ugh_t[gv][1]])
                            for gv in range(2):
                                col = gv * 22 + jf
                                P.op("act", lambda e, gv=gv, hf=hf, col=col: e.activation(
                                    out=cgh[gv][hf][:], in_=pu[gv][hf][:, :], func=AF.Identity, bias=cbc[:, col:col + 1],
                                    scale=cwc[:, 2, col:col + 1]),
                                    reads=pu_t[gv][hf] + [cw_t], writes=[cgh_t[gv][hf]])
                                for tap in (0, 1):
                                    P.op("dve", lambda e, gv=gv, hf=hf, col=col, tap=tap: e.scalar_tensor_tensor(
                                        out=cgh[gv][hf][:], in0=ugh[gv][hf][:, tap:tap + 1024], scalar=cwc[:, tap, col:col + 1],
                                        in1=cgh[gv][hf][:], op0=ALU.mult, op1=ALU.add),
                                        reads=[ugh_t[gv][hf], cw_t, cgh_t[gv][hf]], writes=[cgh_t[gv][hf]])
                            P.op("act", lambda e, hf=hf: e.activation(out=cgh[0][hf][:], in_=cgh[0][hf][:], func=AF.Silu),
                                 reads=[cgh_t[0][hf]], writes=[cgh_t[0][hf]])
                            P.op("dve", lambda e, hf=hf: e.tensor_tensor(
                                out=zo[b][:, hf * 1024:(hf + 1) * 1024], in0=cgh[0][hf][:], in1=cgh[1][hf][:], op=ALU.mult),
                                reads=[cgh_t[0][hf], cgh_t[1][hf]], writes=[zo_t[b][hf]])
                        P.dma("act", zT_d[jf * 128:(jf + 1) * 128, :], zo[b][:], reads=zo_t[b])

                    ffn_prefetch(0)
                    for jf in range(22):
                        ffn_tile(jf)
                    P.barrier()

            with contextlib.ExitStack() as sdn:
                wd = sb("wd", [128, 22, D], BF16, sdn)
                wd_t = [T("wd%d" % j) for j in range(22)]
                dst_ = [sb("dst%d" % i, [128, D], F32, sdn) for i in range(2)]
                dst_t = [T("dst%d" % i) for i in range(2)]
                for jf in range(22):
                    b = jf % 2
                    P.dma(dq[jf % 2], dst_[b][:], wdn_d[l][jf * 128:(jf + 1) * 128, :], writes=[dst_t[b]])
                    P.op("pool" if jf % 2 == 0 else "dve", lambda e, b=b, jf=jf: e.tensor_copy(out=wd[:, jf, :], in_=dst_[b][:]),
                         reads=[dst_t[b]], writes=[wd_t[jf]])
                zc = [sb("zc%d" % i, [128, 22, 512], BF16, sdn) for i in range(2)]
                zc_t = [T("zc%d" % i) for i in range(2)]
                pd = [ps("pd%d" % i, [128, 512], F32, sdn) for i in range(3)]
                pd_t = [T("pd%d" % i) for i in range(3)]
                n = 0
                for c in range(4):
                    cs = slice(c * 512, (c + 1) * 512)
                    b = c % 2
                    P.dma("sp", zc[b][:], zT_d[:, cs].rearrange("(j p) t -> p j t", p=128), writes=[zc_t[b]])
                    for m in range(8):
                        i = n % 3
                        n += 1
                        for jf in range(22):
                            P.op("pe", lambda e, i=i, b=b, m=m, jf=jf: e.matmul(
                                pd[i][:], lhsT=wd[:, jf, m * 128:(m + 1) * 128], rhs=zc[b][:, jf, :],
                                start=(jf == 0), stop=(jf == 21)),
                                reads=[wd_t[jf], zc_t[b]], writes=[pd_t[i]])
                        P.op("dve", lambda e, i=i, m=m, cs=cs: e.scalar_tensor_tensor(
                            out=xT[:, m, cs], in0=pd[i][:], scalar=modT[:, 40 + m:41 + m], in1=xT[:, m, cs],
                            op0=ALU.mult, op1=ALU.add),
                            reads=[pd_t[i], mod_t, xT_t[m][c]], writes=[xT_t[m][c]])
                P.barrier()

            if dbg == "xf" and l == nlayers - 1:
                for j in range(8):
                    P.dma(dq[j % 3], dbg_d[j * 128:(j + 1) * 128, :], xT[:, j, :], reads=xT_t[j])
                P.barrier()
                return True


        for l in range(nlayers):
            if layer_body(l):
                break

        if dbg is None:
            with contextlib.ExitStack() as sfn:
                fo = [sb("fo%d" % i, [128, 512], F32, sfn) for i in range(3)]
                fo_t = [T("fo%d" % i) for i in range(3)]
                cnt = [0]

                def sink_f(j, c, cs, tp, tp_t, g, sh, par_t):
                    i = cnt[0] % 3
                    cnt[0] += 1
                    P.op("act", lambda e: e.activation(out=fo[i][:], in_=tp[:], func=AF.Identity, scale=g),
                         reads=[tp_t] + par_t, writes=[fo_t[i]])
                    P.dma(dq[i % 2], out_d[j * 128:(j + 1) * 128, cs], fo[i][:], reads=[fo_t[i]])

                norm_stage(lambda j: nfin[:, j:j + 1], lambda j: None, [nfin_t], sink_f)

        P.barrier()
        P.emit()
    return nc


_CACHE = {}
_CONST = {}


def _consts():
    if _CONST:
        return _CONST
    bf = ml_dtypes.bfloat16
    s_ = np.arange(128)[:, None]
    q_ = np.arange(512)[None, :]
    cb = np.stack([np.where(j * 128 + s_ <= q_, 0.0, NEG) for j in range(4)], axis=1)
    _CONST["c_cb"] = cb.astype(bf)
    _CONST["c_idb"] = np.eye(128).astype(bf)
    _CONST["c_idf"] = np.eye(128, dtype=np.float32)
    sel = np.zeros((64, 4, 4), np.float32)
    for h in range(4):
        sel[:, h, h] = 1.0
    _CONST["c_sel"] = sel.astype(bf)
    rows = np.zeros((12, S), np.float32)
    rows[0:3] = 1.0
    rows[3:6] = -1.0
    rows[6] = 1.0
    _CONST["c_rows"] = rows.astype(bf)
    oneh = np.zeros((8, S), np.float32)
    for n in range(8):
        oneh[n, n * 256:(n + 1) * 256] = 1.0
    _CONST["c_oneh"] = oneh.astype(bf)
    own = (np.arange(16) // 2)[:, None]
    nn = np.arange(8)[None, :]
    negm = np.where(nn < own, 0.0, -1e30).astype(np.float32)
    past = np.where(nn < own, 1.0, 0.0).astype(np.float32)
    _CONST["c_negm"] = np.ascontiguousarray(np.broadcast_to(negm[None], (128, 16, 8)))
    _CONST["c_past"] = np.ascontiguousarray(np.broadcast_to(past[None], (128, 16, 8)))
    a = np.arange(64)
    mL = (a[:, None] > a[None, :]).astype(np.float32)
    mU = (a[None, :] > a[:, None]).astype(np.float32)
    mUi = (a[None, :] >= a[:, None]).astype(np.float32)
    for nm, m in (("c_mL", mL), ("c_mU", mU), ("c_mUi", mUi)):
        _CONST[nm] = np.ascontiguousarray(np.broadcast_to(m[:, None, :], (64, 8, 64)))
    rst = np.ones((64, 8, SW), np.float32)
    rst[:, :, 0::CH] = 0.0
    _CONST["c_rst"] = rst
    return _CONST


def _hd(v):
    return np.ascontiguousarray(np.asarray(v).reshape(8, 64).T)


def _prep_inputs(inputs, cfg):
    f = lambda a: np.ascontiguousarray(np.asarray(a, dtype=np.float32))
    x = f(inputs["x"])
    c = f(inputs["c"])
    g = {k: f(v) for k, v in inputs.items()}
    rw64 = []
    for l in range(DEPTH):
        mu = g["rwkv_mu"][l]
        parts = [_hd(g["rwkv_w0"][l]), _hd(g["rwkv_a0"][l]), _hd(g["rwkv_k_k"][l]), _hd(g["rwkv_k_a"][l]),
                 _hd(g["rwkv_r_k"][l].reshape(-1)), _hd(g["rwkv_ln_w"][l]), _hd(g["rwkv_ln_b"][l]),
                 _hd(mu[0:512]), _hd(mu[512:1024]), _hd(mu[1024:1536]),
                 mu[1536:1600].reshape(64, 1), mu[1600:1664].reshape(64, 1)]
        rw64.append(np.concatenate(parts, axis=1))
    cw = g["conv_w"]
    shared = {
        "w_mod": g["w_mod"],
        "b_modc": np.stack([_col(g["b_mod"][l]) for l in range(DEPTH)]),
        "norm_mixc": np.stack([_col(g["norm_mix"][l]) for l in range(DEPTH)]),
        "norm_ffnc": np.stack([_col(g["norm_ffn"][l]) for l in range(DEPTH)]),
        "norm_finc": _col(g["norm_final"]),
        "w_in": g["w_in"], "w_out": g["w_out"], "w_up": g["w_up"], "w_down": g["w_down"],
        "conv_wc": np.stack([np.stack([_col(cw[l, j]) for j in range(3)], axis=1) for l in range(DEPTH)]),
        "conv_bc": np.stack([_col(g["conv_b"][l]) for l in range(DEPTH)]),
        "fox_fb": np.ascontiguousarray(g["fox_f_bias"].reshape(DEPTH, 4, 1)),
        "rw64": np.ascontiguousarray(np.stack(rw64)),
        "rw_mug": np.ascontiguousarray(g["rwkv_mu"][:, 1664:1792].reshape(DEPTH, 128, 1)),
        "rwkv_w2": g["rwkv_w2"], "rwkv_a2": g["rwkv_a2"], "rwkv_g2": g["rwkv_g2"],
    }
    shared.update(_consts())
    in_maps = []
    for b in range(8):
        m = dict(shared)
        m["xT"] = np.ascontiguousarray(x[b].T)
        m["cT"] = _col(c[b])
        in_maps.append(m)
    return in_maps


def run(inputs, cfg):
    key = tuple(sorted(cfg.items()))
    if key not in _CACHE:
        _CACHE[key] = build(cfg)
    nc = _CACHE[key]
    in_maps = _prep_inputs(inputs, cfg)
    res = run_bass_kernel_spmd(nc, in_maps, core_ids=list(range(8)))
    return res


def kernel(**inputs):
    res = run(inputs, {})
    out = np.stack([np.ascontiguousarray(np.asarray(r["outT"]).T) for r in res.results], axis=0)
    return out.astype(np.float32)
```
